# Optimizing a Trainium2 kernel written in Bass

```python
import jax, jax.numpy as jnp
from jax import lax
import numpy as np

D_MODEL = 1024
BATCH = 16
SEQ = 2048
DEPTH = 2

N_MIXERS = 2
N_RET_LAYERS = (DEPTH + 1) // 2
N_MLA_LAYERS = DEPTH // 2

D_FF = 4 * D_MODEL
RMS_EPS = 1e-6
GN_EPS = 1e-5
ROPE_THETA = 10000.0
N_ADA = 6

RET_HEADS = D_MODEL // 256
RET_DK = D_MODEL // RET_HEADS
RET_DV = 2 * RET_DK
RET_CHUNK = 128
RET_IN_COLS = 2 * RET_HEADS * RET_DK + 2 * RET_HEADS * RET_DV

MLA_HEADS = D_MODEL // 128
MLA_NOPE = 128
MLA_ROPE = 64
MLA_V = 128
MLA_DQK = MLA_NOPE + MLA_ROPE
MLA_Q_LORA = 3 * D_MODEL // 8
MLA_KV_LORA = D_MODEL // 4
MLA_IN_COLS = MLA_Q_LORA + MLA_KV_LORA + MLA_ROPE
MLA_QBLOCK = 128

kernel_name = "bidir_retention_mla_interleaved_sqrelu_adaln"

F32 = jnp.float32


def rmsnorm(t, w):
    t32 = t.astype(F32)
    y = t32 * lax.rsqrt(jnp.mean(t32 * t32, axis=-1, keepdims=True) + RMS_EPS)
    return (y * w.astype(F32)).astype(t.dtype)


def rope(t, positions):
    d = t.shape[-1]
    inv = ROPE_THETA ** (-jnp.arange(0, d, 2, dtype=F32) / d)
    ang = positions.astype(F32)[..., None] * inv
    cos = jnp.cos(ang)[:, :, None, :]
    sin = jnp.sin(ang)[:, :, None, :]
    t32 = t.astype(F32)
    t1, t2 = t32[..., : d // 2], t32[..., d // 2:]
    return jnp.concatenate([t1 * cos - t2 * sin, t1 * sin + t2 * cos], axis=-1).astype(t.dtype)


def retention_one_direction(q, k, v, log_gamma, strict):
    b, s, h, dk = q.shape
    dv = v.shape[-1]
    n_chunks = s // RET_CHUNK

    def chunks(t):
        return t.reshape(b, n_chunks, RET_CHUNK, h, t.shape[-1]).transpose(1, 0, 3, 2, 4)

    qc, kc, vc = chunks(q), chunks(k), chunks(v)
    pos = jnp.arange(RET_CHUNK, dtype=F32)
    diff = pos[:, None] - pos[None, :]
    mask = (diff > 0) if strict else (diff >= 0)
    lg = log_gamma.astype(F32)
    intra_decay = jnp.where(mask[None], jnp.exp(jnp.where(mask[None], diff[None] * lg[:, None, None], 0.0)), 0.0)
    cross_decay = jnp.exp((pos + 1.0)[None, :] * lg[:, None])[None, :, :, None]
    state_decay = jnp.exp((RET_CHUNK - 1.0 - pos)[None, :] * lg[:, None])[None, :, :, None]
    chunk_decay = jnp.exp(RET_CHUNK * lg)[None, :, None, None]

    def step(state, inp):
        qi, ki, vi = inp
        scores = jnp.einsum('bhnd,bhmd->bhnm', qi, ki) * intra_decay
        out = jnp.einsum('bhnm,bhmv->bhnv', scores, vi)
        out = out + jnp.einsum('bhnd,bhdv->bhnv', qi, state) * cross_decay
        state = state * chunk_decay + jnp.einsum('bhmd,bhmv->bhdv', ki * state_decay, vi)
        return state, out

    state0 = jnp.zeros((b, h, dk, dv), F32)
    _, out = lax.scan(step, state0, (qc, kc, vc))
    return out.transpose(1, 0, 3, 2, 4).reshape(b, s, h, dv)


def retention_mixer(hdn, positions, w_in, logit_fwd, logit_bwd, gn_w, w_out):
    b, s, _ = hdn.shape
    hk = RET_HEADS * RET_DK
    hv = RET_HEADS * RET_DV
    proj = hdn @ w_in
    q, k, v, g = jnp.split(proj, [hk, 2 * hk, 2 * hk + hv], axis=-1)
    q = rope(q.reshape(b, s, RET_HEADS, RET_DK), positions).astype(F32)
    k = rope(k.reshape(b, s, RET_HEADS, RET_DK), positions).astype(F32) * (RET_DK ** -0.5)
    v = v.reshape(b, s, RET_HEADS, RET_DV).astype(F32)
    lg_f = jax.nn.log_sigmoid(logit_fwd.astype(F32))
    lg_b = jax.nn.log_sigmoid(logit_bwd.astype(F32))
    y_f = retention_one_direction(q, k, v, lg_f, strict=False)
    y_b = retention_one_direction(q[:, ::-1], k[:, ::-1], v[:, ::-1], lg_b, strict=True)[:, ::-1]
    y = y_f + y_b
    mu = jnp.mean(y, axis=-1, keepdims=True)
    var = jnp.mean(jnp.square(y - mu), axis=-1, keepdims=True)
    y = ((y - mu) * lax.rsqrt(var + GN_EPS)).reshape(b, s, hv) * gn_w.astype(F32)
    y = jax.nn.silu(g.astype(F32)) * y
    return y.astype(hdn.dtype) @ w_out


def mla_mixer(hdn, positions, w_in, q_norm, w_uq, kv_norm, w_ukv, w_out):
    b, s, _ = hdn.shape
    c_q, c_kv, k_rope = jnp.split(hdn @ w_in, [MLA_Q_LORA, MLA_Q_LORA + MLA_KV_LORA], axis=-1)
    q = (rmsnorm(c_q, q_norm) @ w_uq).reshape(b, s, MLA_HEADS, MLA_DQK)
    q = jnp.concatenate([q[..., :MLA_NOPE], rope(q[..., MLA_NOPE:], positions)], axis=-1)
    kv = (rmsnorm(c_kv, kv_norm) @ w_ukv).reshape(b, s, MLA_HEADS, MLA_NOPE + MLA_V)
    k_nope, v = kv[..., :MLA_NOPE], kv[..., MLA_NOPE:]
    k_rope = rope(k_rope[:, :, None, :], positions)
    k = jnp.concatenate([k_nope, jnp.broadcast_to(k_rope, (b, s, MLA_HEADS, MLA_ROPE))], axis=-1)
    scale = MLA_DQK ** -0.5
    n_blk = s // MLA_QBLOCK
    q_blocks = q.reshape(b, n_blk, MLA_QBLOCK, MLA_HEADS, MLA_DQK).transpose(1, 0, 2, 3, 4)

    def attend(qi):
        sc = jnp.einsum('bqhd,bkhd->bhqk', qi, k, preferred_element_type=F32) * scale
        p = jax.nn.softmax(sc, axis=-1)
        return jnp.einsum('bhqk,bkhd->bqhd', p.astype(v.dtype), v)

    o = lax.map(attend, q_blocks)
    o = o.transpose(1, 0, 2, 3, 4).reshape(b, s, MLA_HEADS * MLA_V)
    return o @ w_out


def setup_inputs(seed: int = 0) -> dict:
    key = jax.random.key(seed)
    ks = jax.random.split(key, 24)

    def w(k, shape, fan_in, s=1.0):
        return jax.random.normal(k, shape, F32) * (s * fan_in ** -0.5)

    def gain(k, shape):
        return 1.0 + 0.05 * jax.random.normal(k, shape, F32)

    base_logit = jnp.asarray(np.log(2.0 ** (5.0 + np.arange(RET_HEADS)) - 1.0), F32)
    x = jax.random.normal(ks[0], (BATCH, SEQ, D_MODEL), F32)
    c = jax.random.normal(ks[1], (BATCH, D_MODEL), F32)
    offsets = jax.random.randint(ks[2], (BATCH, 1), 0, 4096, dtype=jnp.int32)
    positions = offsets + jnp.arange(SEQ, dtype=jnp.int32)[None, :]
    return {
        "x": x,
        "c": c,
        "positions": positions,
        "norm_w": gain(ks[3], (DEPTH, 4, D_MODEL)),
        "ada_w": w(ks[4], (DEPTH, D_MODEL, N_ADA * D_MODEL), D_MODEL, 0.5),
        "ada_b": 0.02 * jax.random.normal(ks[5], (DEPTH, N_ADA * D_MODEL), F32),
        "ret_w_in": w(ks[6], (N_RET_LAYERS, D_MODEL, RET_IN_COLS), D_MODEL),
        "ret_decay_logit_fwd": base_logit + 0.1 * jax.random.normal(ks[7], (N_RET_LAYERS, RET_HEADS), F32),
        "ret_decay_logit_bwd": base_logit + 0.1 * jax.random.normal(ks[8], (N_RET_LAYERS, RET_HEADS), F32),
        "ret_gn_w": gain(ks[9], (N_RET_LAYERS, RET_HEADS * RET_DV)),
        "ret_w_out": w(ks[10], (N_RET_LAYERS, RET_HEADS * RET_DV, D_MODEL), RET_HEADS * RET_DV),
        "mla_w_in": w(ks[11], (N_MLA_LAYERS, D_MODEL, MLA_IN_COLS), D_MODEL),
        "mla_q_norm": gain(ks[12], (N_MLA_LAYERS, MLA_Q_LORA)),
        "mla_w_uq": w(ks[13], (N_MLA_LAYERS, MLA_Q_LORA, MLA_HEADS * MLA_DQK), MLA_Q_LORA),
        "mla_kv_norm": gain(ks[14], (N_MLA_LAYERS, MLA_KV_LORA)),
        "mla_w_ukv": w(ks[15], (N_MLA_LAYERS, MLA_KV_LORA, MLA_HEADS * (MLA_NOPE + MLA_V)), MLA_KV_LORA),
        "mla_w_out": w(ks[16], (N_MLA_LAYERS, MLA_HEADS * MLA_V, D_MODEL), MLA_HEADS * MLA_V),
        "mlp_w1": w(ks[17], (DEPTH, D_MODEL, D_FF), D_MODEL),
        "mlp_w2": w(ks[18], (DEPTH, D_FF, D_MODEL), D_FF),
    }


def reference(x, c, positions, norm_w, ada_w, ada_b, ret_w_in, ret_decay_logit_fwd,
              ret_decay_logit_bwd, ret_gn_w, ret_w_out, mla_w_in, mla_q_norm, mla_w_uq,
              mla_kv_norm, mla_w_ukv, mla_w_out, mlp_w1, mlp_w2):
    c_act = jax.nn.silu(c)
    for i in range(DEPTH):
        mod = c_act @ ada_w[i] + ada_b[i]
        sh_a, sc_a, g_a, sh_m, sc_m, g_m = [m[:, None, :] for m in jnp.split(mod, N_ADA, axis=-1)]
        hdn = rmsnorm(x, norm_w[i, 0]) * (1.0 + sc_a) + sh_a
        j = i // N_MIXERS
        if i % N_MIXERS == 0:
            y = retention_mixer(hdn, positions, ret_w_in[j], ret_decay_logit_fwd[j],
                                ret_decay_logit_bwd[j], ret_gn_w[j], ret_w_out[j])
        else:
            y = mla_mixer(hdn, positions, mla_w_in[j], mla_q_norm[j], mla_w_uq[j],
                          mla_kv_norm[j], mla_w_ukv[j], mla_w_out[j])
        x = x + g_a * rmsnorm(y, norm_w[i, 1])
        hdn = rmsnorm(x, norm_w[i, 2]) * (1.0 + sc_m) + sh_m
        y = jnp.square(jax.nn.relu(hdn @ mlp_w1[i])) @ mlp_w2[i]
        x = x + g_m * rmsnorm(y, norm_w[i, 3])
    return x
```

```python
import contextlib
import math
import numpy as np
import concourse.bass as bass
import concourse.mybir as mybir
from concourse.bass_utils import run_bass_kernel_spmd

DT = mybir.dt
F32 = DT.float32
BF16 = DT.bfloat16
I32 = DT.int32
AF = mybir.ActivationFunctionType
ALU = mybir.AluOpType

_ESZ = {F32: 4, BF16: 2, I32: 4}
BLK = 64
S = 2048
D = 1024
NG = 4
GT = 512


def esize(dtype):
    return _ESZ[dtype]


def footprint(ap):
    space = str(ap.space)
    if space not in ("SB", "PSUM"):
        return None
    es = esize(ap.dtype)
    dims = ap.ap
    pstep = dims[0][0]
    off = ap.offset
    foff = (off % pstep) * es if pstep > 0 else off * es
    free = [(s * es, c) for (s, c) in dims[1:]]
    if not free:
        ivs = [(foff, foff + es)]
    else:
        inner_s, inner_c = free[-1]
        run = (inner_c - 1) * abs(inner_s) + es
        outer = free[:-1]
        n_outer = 1
        for s, c in outer:
            n_outer *= c
        if n_outer > 256:
            lo = foff
            hi = foff + es
            for s, c in free:
                if s >= 0:
                    hi += s * (c - 1)
                else:
                    lo += s * (c - 1)
            ivs = [(lo, hi)]
        else:
            starts = [foff]
            for s, c in outer:
                starts = [st + s * i for st in starts for i in range(c)]
            base = 0 if inner_s >= 0 else inner_s * (inner_c - 1)
            ivs = [(st + base, st + base + run) for st in starts]
    blk = 2048 if space == "PSUM" else BLK
    blocks = set()
    for lo, hi in ivs:
        blocks.update(range(lo // blk, (hi - 1) // blk + 1))
    return (space, blocks)


class Op:
    __slots__ = ("eng", "fn", "idx", "deps", "signals", "val", "kind", "semkey")


class Rec:
    ENGINES = ("tensor", "scalar", "vector", "gpsimd", "sync")

    def __init__(self, nc):
        self.nc = nc
        self.ops = {e: [] for e in self.ENGINES}
        self.last_w = {}
        self.readers = {}
        self.dma_count = {}
        self.waited = {e: {} for e in self.ENGINES}

    def _add(self, eng, fn, reads, writes, kind="c", semkey=None, extra_deps=()):
        op = Op()
        op.eng, op.fn, op.kind, op.semkey = eng, fn, kind, semkey
        op.signals = False
        op.val = None
        op.idx = len(self.ops[eng])
        rfp = [f for f in (footprint(a) for a in reads) if f is not None]
        wfp = [f for f in (footprint(a) for a in writes) if f is not None]
        wfp = wfp + [f for f in rfp if f[0] == "PSUM"]
        deps = []
        rawset = set()
        for space, blocks in rfp:
            for b in blocks:
                lw = self.last_w.get((space, b))
                if lw is not None:
                    deps.append(lw)
                    rawset.add(id(lw))
        for space, blocks in wfp:
            for b in blocks:
                k = (space, b)
                lw = self.last_w.get(k)
                if lw is not None:
                    deps.append(lw)
                rd = self.readers.get(k)
                if rd:
                    deps.extend(rd.values())
        for p in extra_deps:
            deps.append(p)
            rawset.add(id(p))
        best = {}
        for p in deps:
            if p is op:
                continue
            if p.kind == "c":
                if p.eng == eng:
                    if eng == "tensor":
                        continue
                stream = ("c", p.eng)
                cur = best.get(stream)
                if cur is None or p.idx > cur.idx:
                    best[stream] = p
            else:
                stream = ("d", p.semkey)
                cur = best.get(stream)
                if cur is None or p.val > cur.val:
                    best[stream] = p
        final = []
        w = self.waited[eng]
        for stream, p in best.items():
            key = p.idx if p.kind == "c" else p.val
            if w.get(stream, -1) >= key:
                continue
            w[stream] = key
            p.signals = True
            final.append(p)
        op.deps = final
        if kind == "d":
            self.dma_count[semkey] = self.dma_count.get(semkey, 0) + 16
            op.val = self.dma_count[semkey]
        for space, blocks in rfp:
            for b in blocks:
                self.readers.setdefault((space, b), {})[(eng, kind, semkey)] = op
        for space, blocks in wfp:
            for b in blocks:
                k = (space, b)
                self.last_w[k] = op
                if k in self.readers:
                    self.readers[k] = {}
        self.ops[eng].append(op)
        return op

    def c(self, eng, fn, reads, writes, **kw):
        return self._add(eng, fn, reads, writes, kind="c", **kw)

    def dma(self, eng, out, in_, semkey, **kw):
        def fn(e, out=out, in_=in_):
            return e.dma_start(out=out, in_=in_)
        return self._add(eng, fn, [in_], [out], kind="d", semkey=semkey, **kw)

    def emit(self):
        nc = self.nc
        for e in self.ENGINES:
            cnt = 0
            for op in self.ops[e]:
                if op.kind == "c" and op.signals:
                    cnt += 1
                    op.val = cnt
        sems = {}
        with contextlib.ExitStack() as st:
            for e in self.ENGINES:
                sems[("c", e)] = st.enter_context(nc.semaphore("s_" + e))
            for k in self.dma_count:
                sems[("d", k)] = st.enter_context(nc.semaphore("d_" + str(k)))
            block = st.enter_context(nc.Block())

            def make(e):
                def body(eng):
                    for op in self.ops[e]:
                        for p in op.deps:
                            stream = ("c", p.eng) if p.kind == "c" else ("d", p.semkey)
                            eng.wait_ge(sems[stream], p.val)
                        ins = op.fn(eng)
                        if op.kind == "d":
                            ins.then_inc(sems[("d", op.semkey)], 16)
                        elif op.signals:
                            ins.then_inc(sems[("c", e)], 1)
                    if e == "sync":
                        for k, v in self.dma_count.items():
                            eng.wait_ge(sems[("d", k)], v)
                return body
            for e in self.ENGINES:
                getattr(block, e)(make(e))


class Arena:
    def __init__(self, handle, nbytes):
        self.h = handle
        self.nbytes = nbytes
        self.top = 0

    def view(self, off, shape, dtype):
        n = 1
        for s in shape:
            n *= s
        nb = n * esize(dtype)
        assert off % 4 == 0 and off + nb <= self.nbytes, (off, nb, self.nbytes)
        ap = self.h[:, off // 4:(off + nb + 3) // 4]
        if dtype != F32:
            ap = ap.bitcast(dtype)
        if len(shape) == 2:
            ap = ap.rearrange("p (a b) -> p a b", a=shape[0])
        elif len(shape) == 3:
            ap = ap.rearrange("p (a b c) -> p a b c", a=shape[0], b=shape[1])
        return ap

    def alloc(self, shape, dtype):
        n = 1
        for s in shape:
            n *= s
        nb = (n * esize(dtype) + 63) // 64 * 64
        off = self.top
        self.top += nb
        return self.view(off, shape, dtype)


class Reg:
    def __init__(self, arena, base, size):
        self.a, self.base, self.size, self.top = arena, base, size, 0

    def alloc(self, shape, dtype):
        n = 1
        for s in shape:
            n *= s
        nb = (n * esize(dtype) + 63) // 64 * 64
        assert self.top + nb <= self.size, ("region overflow", self.top, nb, self.size)
        v = self.a.view(self.base + self.top, shape, dtype)
        self.top += nb
        return v


ARENA_BYTES = 211968
RX_BYTES = 64 * 1024
RA_BYTES = 113 * 1024


def build(layers=(0, 1), nseq=2, dbg=None):
    nc = bass.Bass("TRN2", target_bir_lowering=False)

    def din(name, shape, dtype=F32):
        return nc.dram_tensor(name, list(shape), dtype, kind="ExternalInput")
    xT_d = din("xT", [2, 128, 8, S])
    cT_d = din("cT", [128, 8, 2])
    pos_d = din("pos", [2, S], I32)
    ctab_d = din("ctab", [128, 4 * 128 + 8])
    adar_d = din("adar", [2, 12, 128, 8 * 512])
    adab_d = din("adab", [2, 128, 48 * 2])
    normw_d = din("normw", [2, 128, 32])
    retwin_d = din("retwin", [4, 128, 8 * 1536])
    retwout_d = din("retwout", [8, 128, 16 * 128])
    gnw_d = din("gnw", [128, 16])
    declog_d = din("declog", [1, 8])
    w1r_d = din("w1r", [2, 8, 128, 8 * 512])
    w2r_d = din("w2r", [2, 8, 128, 32 * 128])
    mwin_d = din("mwin", [128, 8 * 704])
    mwuq_d = din("mwuq", [128, 3 * 1536])
    mwukv_d = din("mwukv", [128, 2 * 2048])
    mwout_d = din("mwout", [128, 8 * 1024])
    mnorm_d = din("mnorm", [128, 8])
    out_d = nc.dram_tensor("outT", [2, 128, 8, S], F32, kind="ExternalOutput")
    w1s_d = nc.dram_tensor("w1s", [2, 8, 128, 8 * 512], BF16, kind="Internal")
    w2s_d = nc.dram_tensor("w2s", [2, 8, 128, 32 * 128], BF16, kind="Internal")
    ygs_d = nc.dram_tensor("ygs", [4, 128, 4, S], BF16, kind="Internal")

    with contextlib.ExitStack() as st:
        ah = st.enter_context(nc.sbuf_tensor("arena", [128, ARENA_BYTES // 4], F32))
        ph = st.enter_context(nc.psum_tensor("psum", [128, 4096], F32))
        A = Arena(ah, ARENA_BYTES)
        r = Rec(nc)

        def ps(b, n=512):
            return ph[:, b * 512:b * 512 + n]

        def psb(b):
            return ph[:, b * 512:(b + 1) * 512].bitcast(BF16)

        def mm(out, lhsT, rhs, start, stop):
            r.c("tensor", lambda e: e.matmul(out, lhsT=lhsT, rhs=rhs, start=start, stop=stop), [lhsT, rhs], [out])

        def tr(out, in_, ident):
            r.c("tensor", lambda e: e.transpose(out, in_, ident), [in_, ident], [out])

        def act(out, in_, func, scale=1.0, bias=None, accum=None, eng="scalar"):
            rd = [in_]
            kw = {}
            if bias is not None:
                kw["bias"] = bias
                rd.append(bias)
            if not isinstance(scale, (int, float)):
                rd.append(scale)
            wr = [out]
            if accum is not None:
                kw["accum_out"] = accum
                wr.append(accum)
            r.c("scalar", lambda e: e.activation(out=out, in_=in_, func=func, scale=scale, **kw), rd, wr)

        def tt(eng, out, in0, in1, op):
            r.c(eng, lambda e: e.tensor_tensor(out=out, in0=in0, in1=in1, op=op), [in0, in1], [out])

        def ts(eng, out, in0, s1, s2, op0, op1=None):
            rd = [in0] + [s for s in (s1, s2) if s is not None and not isinstance(s, (int, float))]
            if op1 is None:
                r.c(eng, lambda e: e.tensor_scalar(out=out, in0=in0, scalar1=s1, scalar2=None, op0=op0), rd, [out])
            else:
                r.c(eng, lambda e: e.tensor_scalar(out=out, in0=in0, scalar1=s1, scalar2=s2, op0=op0, op1=op1), rd, [out])

        def stt(eng, out, in0, sc, in1, op0, op1):
            rd = [in0, in1] + ([] if isinstance(sc, (int, float)) else [sc])
            r.c(eng, lambda e: e.scalar_tensor_tensor(out=out, in0=in0, scalar=sc, in1=in1, op0=op0, op1=op1), rd, [out])

        def cp(eng, out, in_):
            if eng == "scalar":
                r.c(eng, lambda e: e.copy(out=out, in_=in_), [in_], [out])
            else:
                r.c(eng, lambda e: e.tensor_copy(out=out, in_=in_), [in_], [out])

        def recip(out, in_):
            r.c("vector", lambda e: e.reciprocal(out=out, in_=in_), [in_], [out])

        def memset(eng, out, v):
            r.c(eng, lambda e: e.memset(out, v), [], [out])

        ones = A.alloc([128], BF16)
        ident = A.alloc([128], BF16)
        ctab = A.alloc([4 * 128 + 8], F32)
        eps6 = A.alloc([1], F32)
        eps5 = A.alloc([1], F32)
        negpi = A.alloc([1], F32)
        cst = A.alloc([2], F32)
        cact = A.alloc([8, 2], BF16)
        cin = A.alloc([8, 2], F32)
        modt = [A.alloc([48, 2], F32) for _ in range(2)]
        adab = [A.alloc([48, 2], F32) for _ in range(2)]
        normw = [A.alloc([4, 8], F32) for _ in range(2)]
        prm = [[A.alloc([6, 8], F32) for _ in range(2)] for _ in range(2)]
        lg = A.alloc([8], F32)
        lgt = A.alloc([8], F32)
        Dm = A.alloc([4, 128], F32)
        cdfb = A.alloc([8, 128], F32)
        sd = A.alloc([8], F32)
        cdec = A.alloc([8], F32)
        mnorm = A.alloc([8], F32)
        gnwf = A.alloc([16], F32)
        stat = A.alloc([32], F32)
        scr = [A.alloc([512], F32) for _ in range(4)]
        rs = A.alloc([512], F32)
        sqb = A.alloc([8, 512], BF16)
        misc_end = A.top
        RX = misc_end
        RA = RX + RX_BYTES
        assert RA + RA_BYTES <= ARENA_BYTES, (RA + RA_BYTES, ARENA_BYTES)
        xT = A.view(RX, [8, S], F32)

        diffpos = ctab[:, 0:128]
        diffneg = ctab[:, 128:256]
        np1 = ctab[:, 256:384]
        n128m = ctab[:, 384:512]
        c127mp = ctab[:, 512:513]
        cp_ = ctab[:, 513:514]
        invf_ret = ctab[:, 514:515]
        invf_mla = ctab[:, 515:516]
        identf = None

        memset("vector", ones, 1.0)
        memset("vector", eps6, 1e-6)
        memset("vector", eps5, 1e-5)
        memset("vector", negpi, -math.pi)
        memset("vector", cst[:, 0:1], -0.5)
        memset("vector", cst[:, 1:2], -1.0)
        r.dma("sync", ctab, ctab_d.ap(), "c0")
        r.dma("sync", cin, cT_d.ap().rearrange("p (a b) -> p a b", a=8) if False else cT_d.ap(), "c1")
        r.dma("sync", mnorm, mnorm_d.ap(), "c2")
        r.dma("sync", gnwf, gnw_d.ap(), "c6")
        for l in range(2):
            r.dma("sync", adab[l], adab_d.ap()[l].rearrange("p (a b) -> p a b", a=48), "c3_%d" % l)
            r.dma("sync", normw[l], normw_d.ap()[l].rearrange("p (a b) -> p a b", a=4), "c4_%d" % l)
        r.dma("sync", lgt, declog_d.ap().partition_broadcast(128), "c5")
        tt("vector", scr[0][:, 0:128], diffpos, diffneg, ALU.add)
        ts("vector", ident, scr[0][:, 0:128], 0.0, None, ALU.is_equal)
        act(cact, cin, AF.Silu)

        d2d = {}

        def emit_d2d(l):
            for s_ in range(8):
                d2d[("w1", l, s_)] = r.dma("gpsimd", w1s_d.ap()[l, s_], w1r_d.ap()[l, s_], "d2dw1_%d" % l)
                d2d[("w2", l, s_)] = r.dma("gpsimd", w2s_d.ap()[l, s_], w2r_d.ap()[l, s_], "d2dw2_%d" % l)

        ai_ = [0]

        def emit_ada(l):
            RAr = Reg(A, RA, RA_BYTES)
            adabuf = [RAr.alloc([8, 512], BF16) for _ in range(2)]
            for s_ in range(12):
                buf = adabuf[ai_[0] % 2]
                r.dma("gpsimd", buf, adar_d.ap()[l, s_].rearrange("p (a b) -> p a b", a=8), "ada%d" % (ai_[0] % 2))
                ai_[0] += 1
                pm = ps(7, 8)
                for j4 in range(4):
                    for kc in range(8):
                        mm(pm[:, j4 * 2:j4 * 2 + 2], buf[:, kc, j4 * 128:(j4 + 1) * 128], cact[:, kc, :], kc == 0, kc == 7)
                tt("vector", modt[l][:, s_ * 4:(s_ + 1) * 4, :], pm.rearrange("p (a b) -> p a b", a=4),
                   adab[l][:, s_ * 4:(s_ + 1) * 4, :], ALU.add)
            ada_params(l)

        def ada_small(l, j, base):
            buf = A.view(base + (j % 2) * 2048, [8, 128], BF16)
            s_, j4 = j // 4, j % 4
            r.dma("gpsimd", buf, adar_d.ap()[l, s_].rearrange("p (a b) -> p a b", a=8)[:, :, j4 * 128:(j4 + 1) * 128],
                  "adas%d" % (j % 2))
            pm = ps(7, 2)
            for kc in range(8):
                mm(pm, buf[:, kc, :], cact[:, kc, :], kc == 0, kc == 7)
            tt("vector", modt[l][:, j, :], pm, adab[l][:, j, :], ALU.add)

        def ada_params(l):
            for b in range(2):
                P = prm[l][b]
                m = modt[l]
                stt("vector", P[:, 0, :], m[:, 8:16, b], 1.0, normw[l][:, 0, :], ALU.add, ALU.mult)
                cp("vector", P[:, 1, :], m[:, 0:8, b])
                tt("vector", P[:, 2, :], m[:, 16:24, b], normw[l][:, 1, :], ALU.mult)
                stt("vector", P[:, 3, :], m[:, 32:40, b], 1.0, normw[l][:, 2, :], ALU.add, ALU.mult)
                cp("vector", P[:, 4, :], m[:, 24:32, b])
                tt("vector", P[:, 5, :], m[:, 40:48, b], normw[l][:, 3, :], ALU.mult)

        hoisted_rope = [False]
        if 0 in layers:
            pass

        if 0 in layers:
            act(lg, lgt, AF.Exp, scale=-1.0)
            ts("vector", lg, lg, 1.0, None, ALU.add)
            act(lg, lg, AF.Ln)
            ts("vector", lg, lg, -1.0, None, ALU.mult)
            for h in range(4):
                t1 = scr[0][:, 0:128]
                ts("vector", t1, diffpos, lg[:, h:h + 1], None, ALU.mult)
                stt("vector", t1, diffneg, lg[:, 4 + h:5 + h], t1, ALU.mult, ALU.add)
                act(Dm[:, h, :], t1, AF.Exp)
                act(cdfb[:, h, :], np1, AF.Exp, scale=lg[:, h:h + 1])
                act(cdfb[:, 4 + h, :], n128m, AF.Exp, scale=lg[:, 4 + h:5 + h])
                act(sd[:, h:h + 1], c127mp, AF.Exp, scale=lg[:, h:h + 1])
                act(sd[:, 4 + h:5 + h], cp_, AF.Exp, scale=lg[:, 4 + h:5 + h])
            act(cdec, lg, AF.Exp, scale=128.0)

        bank_rot = [0]

        def nb(lo=0, hi=5):
            b = lo + bank_rot[0] % (hi - lo + 1)
            bank_rot[0] += 1
            return b

        def rstd_from_sq(nch, dim, sq=None):
            sq = sqb if sq is None else sq
            p7 = ps(7)
            for c in range(nch):
                mm(p7, ones, sq[:, c, :], c == 0, c == nch - 1)
            act(rs, p7, AF.Ln, scale=1.0 / dim, bias=eps6)
            act(rs, rs, AF.Exp, scale=-0.5)

        def prenorm_a(src):
            for c in range(8):
                act(sqb[:, c, :], src(c), AF.Square)
            rstd_from_sq(8, D)

        def prenorm_b(src, Aa, Bb, hdn):
            for c in range(8):
                t = scr[c % 2]
                tt("vector", t, src(c), rs, ALU.mult)
                act(hdn[:, c, :], t, AF.Identity, scale=Aa[:, c:c + 1], bias=Bb[:, c:c + 1])

        def prenorm(src, Aa, Bb, hdn):
            prenorm_a(src)
            prenorm_b(src, Aa, Bb, hdn)

        def postnorm(ybuf, G, xin, xout, sq=None):
            rstd_from_sq(8, D, sq)
            for c in range(8):
                t = scr[c % 2]
                tt("vector", t, ybuf[:, c, :], rs, ALU.mult)
                stt("vector", xout(c), t, G[:, c:c + 1], xin(c), ALU.mult, ALU.add)

        def rope_prep(b, base):
            ti = A.view(base, [S], I32)
            r.dma("sync", ti, pos_d.ap()[b:b + 1, :].partition_broadcast(128), "pos")

        def rope_slice(invf, npart, cosT, sinT, base, gs):
            P_ = slice(0, npart)
            ti = A.view(base, [S], I32)[P_, gs]
            tg = A.view(base, [S], F32)[P_, gs]
            ua = A.view(base + 8192, [S], F32)[P_, gs]
            tf = A.view(base + 16384, [S], F32)[P_, gs]
            cp("vector", ua, ti)
            ts("vector", ua, ua, invf[P_], 1.0 / (2 * math.pi), ALU.mult, ALU.mult)
            for shift, dst in ((0.0, sinT), (0.25, cosT)):
                ts("vector", tf, ua, shift, None, ALU.add)
                cp("vector", ti, tf)
                cp("vector", tg, ti)
                tt("vector", tf, tf, tg, ALU.subtract)
                ts("vector", tg, tf, 0.5, None, ALU.is_gt)
                tt("vector", tf, tf, tg, ALU.subtract)
                ts("vector", tg, tf, -0.5, None, ALU.is_lt)
                tt("vector", tf, tf, tg, ALU.add)
                act(dst[:, gs], tf, AF.Sin, scale=2 * math.pi)

        def rope_tables(b, invf, npart, cosT, sinT, base):
            rope_prep(b, base)
            for g_ in range(NG):
                rope_slice(invf, npart, cosT, sinT, base, slice(g_ * GT, (g_ + 1) * GT))

        if 0 in layers:
            rope_tables(0, invf_ret, 128, A.view(RX + 32768, [S], F32), A.view(RX + 40960, [S], F32), RA + 32768)
            hoisted_rope[0] = True
        emit_ada(layers[0])

        def mlp(l, b):
            P = prm[l][b]
            R = Reg(A, RA, RA_BYTES)
            hT = R.alloc([32, 512], BF16)
            ring_base = RA + R.top
            ring = [R.alloc([4096], BF16) for _ in range(4)]
            ring_i = [0]

            def next_slab(shape):
                k = ring_i[0] % 4
                ring_i[0] += 1
                return A.view(ring_base + k * 8192, shape, BF16), k
            ybuf = R.alloc([8, 512], F32)
            hdns = [R.alloc([8, 512], BF16) for _ in range(2)]
            ysq = R.alloc([8, 512], BF16)

            def xsrc(g):
                gs = slice(g * GT, (g + 1) * GT)
                return lambda c: xT[:, c, gs]
            prenorm(xsrc(0), P[:, 3, :], P[:, 4, :], hdns[0])
            for g in range(NG):
                gs = slice(g * GT, (g + 1) * GT)
                hdn = hdns[g % 2]
                for s in range(8):
                    wb, k_ = next_slab([8, 512])
                    r.dma("sync", wb, w1s_d.ap()[l, s].rearrange("p (a b) -> p a b", a=8), "wring%d" % k_,
                          extra_deps=[d2d[("w1", l, 7)]])
                    for j4 in range(4):
                        j = s * 4 + j4
                        pb = ps(nb())
                        for kc in range(8):
                            mm(pb, wb[:, kc, j4 * 128:(j4 + 1) * 128], hdn[:, kc, :], kc == 0, kc == 7)
                        t = scr[2 + j % 2]
                        act(t, pb, AF.Relu)
                        tt("vector", hT[:, j, :], pb, t, ALU.mult)
                    if s == 1 and g + 1 < NG:
                        prenorm_a(xsrc(g + 1))
                    if l == 0 and b == 0 and 1 in layers:
                        p_ = g * 8 + s
                        for j in range(p_ * 3 // 2, (p_ + 1) * 3 // 2):
                            ada_small(1, j, RA + R.top)
                        if p_ == 31:
                            ada_params(1)
                if g + 1 < NG:
                    prenorm_b(xsrc(g + 1), P[:, 3, :], P[:, 4, :], hdns[(g + 1) % 2])
                for m in range(8):
                    wb, k_ = next_slab([32, 128])
                    r.dma("sync", wb, w2s_d.ap()[l, m].rearrange("p (a b) -> p a b", a=32), "wring%d" % k_,
                          extra_deps=[d2d[("w2", l, 7)]])
                    pb = ps(nb())
                    for j in range(32):
                        mm(pb, wb[:, j, :], hT[:, j, :], j == 0, j == 31)
                    act(ybuf[:, m, :], pb, AF.Copy)
                    act(ysq[:, m, :], pb, AF.Square)
                postnorm(ybuf, P[:, 5, :], lambda c: xT[:, c, gs], lambda c: xT[:, c, gs], sq=ysq)
                if l == layers[-1] and dbg is None:
                    r.dma("gpsimd", out_d.ap()[b, :, :, gs], xT[:, :, gs], "ost%d" % g)

        def retention(b):
            P = prm[0][b]
            RXr = Reg(A, RX, RX_BYTES)
            hdnT = RXr.alloc([8, S], BF16)
            cosT = RXr.alloc([S], F32)
            sinT = RXr.alloc([S], F32)
            xs = [RXr.alloc([8, 256], F32) for _ in range(2)]
            R = Reg(A, RA, RA_BYTES)
            wh = R.alloc([8, 1024], BF16)
            wg = R.alloc([8, 512], BF16)
            qT = R.alloc([2, S], BF16)
            kT = R.alloc([2, S], BF16)
            ktm = R.alloc([16, 256], BF16)
            vtm = R.alloc([16, 512], BF16)
            Sbs = R.alloc([16, 2, 512], BF16)
            S32 = R.alloc([2, 512], F32)
            Sb32 = S32
            Sfb = [R.alloc([2, 512], BF16) for _ in range(2)]
            sT = [R.alloc([128], BF16) for _ in range(2)]
            qf = [R.alloc([2, 128], BF16) for _ in range(2)]
            qb = [R.alloc([2, 128], BF16) for _ in range(2)]
            kf = [R.alloc([256], BF16) for _ in range(3)]
            ygt = [R.alloc([512], BF16) for _ in range(2)]
            ygT = A.view(RX + 48 * 1024, [4, S], BF16)
            rope_inline = not (b == 0 and hoisted_rope[0])
            if rope_inline:
                rope_prep(b, RA + 32768)
            for g in range(NG):
                for c in range(8):
                    pass
                for hf in range(2):
                    t0 = g * GT + hf * 256
                    r.dma("sync", xs[hf], xT_d.ap()[b, :, :, t0:t0 + 256], "xs%d" % hf)
                    for c in range(8):
                        act(sqb[:, c, hf * 256:(hf + 1) * 256], xs[hf][:, c, :], AF.Square)
                rstd_from_sq(8, D)
                if rope_inline:
                    rope_slice(invf_ret, 128, cosT, sinT, RA + 32768, slice(g * GT, (g + 1) * GT))
                for hf in range(2):
                    t0 = g * GT + hf * 256
                    for c in range(8):
                        t = scr[c % 2][:, 0:256]
                        tt("vector", t, xs[hf][:, c, :], rs[:, hf * 256:(hf + 1) * 256], ALU.mult)
                        act(hdnT[:, c, t0:t0 + 256], t, AF.Identity, scale=P[:, 0, c:c + 1], bias=P[:, 1, c:c + 1])
            if dbg == "hdn0":
                return hdnT
            def emit_wh(h_):
                r.dma("gpsimd", wh, retwin_d.ap()[h_].rearrange("p (a b) -> p a b", a=8)[:, :, 0:1024], "wh")

            emit_wh(0)
            for h in range(4):
                r.dma("gpsimd", wg, retwin_d.ap()[h].rearrange("p (a b) -> p a b", a=8)[:, :, 1024:1536], "wg")
                memset("gpsimd", Sb32, 0.0)
                memset("gpsimd", Sbs[:, 15, :, :], 0.0)

                def kfb_emit(c):
                    if c >= 1:
                        ts("vector", kf[c % 3], ktm[:, c, :], sd[:, 4 + h:5 + h], None, ALU.mult)

                def bstep(c, nxt):
                    kfb = kf[c % 3]
                    pbs = (ps(nb()), ps(nb()))
                    for dc in range(2):
                        mm(pbs[dc], kfb[:, dc * 128:(dc + 1) * 128], vtm[:, c, :], True, True)
                    if nxt is not None:
                        kfb_emit(nxt)
                    for dc in range(2):
                        stt("vector", Sb32[:, dc, :], Sb32[:, dc, :], cdec[:, 4 + h:5 + h], pbs[dc], ALU.mult, ALU.add)
                    cp("scalar", Sbs[:, c - 1, :, :], Sb32)

                for g in (3, 2, 1, 0):
                    gs = slice(g * GT, (g + 1) * GT)
                    for which, dst, kscale in ((0, qT, 1.0), (1, kT, 1.0 / 16.0)):
                        pa, pbk = ps(nb()), ps(nb())
                        for dc, pp in ((0, pa), (1, pbk)):
                            col = which * 256 + dc * 128
                            for kc in range(8):
                                mm(pp, wh[:, kc, col:col + 128], hdnT[:, kc, gs], kc == 0, kc == 7)
                        stt("vector", scr[0], pa, kscale, cosT[:, gs], ALU.mult, ALU.mult)
                        stt("vector", scr[1], pbk, kscale, sinT[:, gs], ALU.mult, ALU.mult)
                        tt("gpsimd", dst[:, 0, gs], scr[0], scr[1], ALU.subtract)
                        stt("vector", scr[2], pa, kscale, sinT[:, gs], ALU.mult, ALU.mult)
                        stt("vector", scr[3], pbk, kscale, cosT[:, gs], ALU.mult, ALU.mult)
                        tt("gpsimd", dst[:, 1, gs], scr[2], scr[3], ALU.add)
                    for ti in range(4):
                        t = g * 4 + ti
                        tsl = slice(t * 128, (t + 1) * 128)
                        pv = ps(nb())
                        for kc in range(8):
                            mm(pv, hdnT[:, kc, tsl], wh[:, kc, 512:1024], kc == 0, kc == 7)
                        act(vtm[:, t, :], pv, AF.Copy)
                        p6 = psb(6)
                        for dc in range(2):
                            tr(p6[:, dc * 128:(dc + 1) * 128], kT[:, dc, tsl], ident)
                        cp("vector", ktm[:, t, :], p6[:, 0:256])
                    cs = [c for c in range(4 * g + 3, 4 * g - 1, -1) if c >= 1]
                    kfb_emit(cs[0])
                    if len(cs) > 1:
                        kfb_emit(cs[1])
                    for i_, c in enumerate(cs):
                        bstep(c, cs[i_ + 2] if i_ + 2 < len(cs) else None)
                if h + 1 < 4:
                    emit_wh(h + 1)
                if b == 0:
                    for s_ in (2 * h, 2 * h + 1):
                        d2d[("w1", 0, s_)] = r.dma("gpsimd", w1s_d.ap()[0, s_], w1r_d.ap()[0, s_], "d2dw1_0")
                        d2d[("w2", 0, s_)] = r.dma("gpsimd", w2s_d.ap()[0, s_], w2r_d.ap()[0, s_], "d2dw2_0")
                memset("gpsimd", S32, 0.0)
                ybs = [scr[0], scr[1], scr[2]]
                yas = [scr[3], rs]
                pos_ = {}

                def tsl_(c):
                    return slice(c * 128, (c + 1) * 128)

                def kf_emit(c):
                    if c < 15:
                        ts("vector", kf[c % 2], ktm[:, c, :], sd[:, h:h + 1], None, ALU.mult)

                kf_emit(0)
                for i in range(16 + 3):
                    c = i
                    c1, c2, c3 = i - 1, i - 2, i - 3
                    if 0 <= c2 < 16:
                        p2 = c2 % 2
                        po2 = pos_[c2]
                        st_ = stat[:, p2 * 16:(p2 + 1) * 16]
                        r.c("vector", lambda e, po=po2, st_=st_: e.bn_stats(out=st_[:, 0:6], in_=po), [po2], [st_[:, 0:6]])
                        r.c("vector", lambda e, st_=st_: e.bn_aggr(out=st_[:, 6:8], in_=st_[:, 0:6]), [st_[:, 0:6]], [st_[:, 6:8]])
                        tt("gpsimd", st_[:, 8:9], st_[:, 7:8], eps5, ALU.add)
                        tt("gpsimd", st_[:, 9:10], st_[:, 8:9], cst[:, 0:1], ALU.pow)
                        tt("gpsimd", st_[:, 11:12], st_[:, 6:7], cst[:, 1:2], ALU.mult)
                        tt("gpsimd", st_[:, 10:11], st_[:, 11:12], st_[:, 9:10], ALU.mult)
                    if c < 16:
                        par = c % 2
                        tsl = tsl_(c)
                        if c < 15:
                            for dc in range(2):
                                mm(ps(dc), kf[par][:, dc * 128:(dc + 1) * 128], vtm[:, c, :], True, True)
                        pS = ps(2)[:, 0:128]
                        for dc in range(2):
                            mm(pS, kT[:, dc, tsl], qT[:, dc, tsl], dc == 0, dc == 1)
                        pg = ps(3)
                        for kc in range(8):
                            mm(pg, hdnT[:, kc, tsl], wg[:, kc, :], kc == 0, kc == 7)
                    if 0 <= c1 < 16:
                        p1 = c1 % 2
                        po = ps((4, 5, 7)[c1 % 3])
                        pos_[c1] = po
                        ops_ = [(sT[p1], vtm[:, c1, :])]
                        if c1 > 0:
                            ops_ += [(qf[p1][:, 0, :], Sfb[p1][:, 0, :]), (qf[p1][:, 1, :], Sfb[p1][:, 1, :])]
                        if c1 < 15:
                            ops_ += [(qb[p1][:, 0, :], Sbs[:, c1, 0, :]), (qb[p1][:, 1, :], Sbs[:, c1, 1, :])]
                        for k_, (lt, rh) in enumerate(ops_):
                            mm(po, lt, rh, k_ == 0, k_ == len(ops_) - 1)
                    if 0 <= c3 < 16:
                        p6 = psb(6)
                        for j in range(4):
                            tr(p6[:, j * 128:(j + 1) * 128], ygt[c3 % 2][:, j * 128:(j + 1) * 128], ident)
                    if 0 <= c2 < 16:
                        act(yas[p2], po2, AF.Identity, scale=st_[:, 9:10], bias=st_[:, 10:11])
                    if c < 16:
                        if c < 15:
                            for dc in range(2):
                                stt("vector", S32[:, dc, :], S32[:, dc, :], cdec[:, h:h + 1], ps(dc), ALU.mult, ALU.add)
                            cp("scalar", Sfb[1 - par], S32)
                        tt("vector", sT[par], pS, Dm[:, h, :], ALU.mult)
                        for dc in range(2):
                            if c > 0:
                                tt("gpsimd", qf[par][:, dc, :], qT[:, dc, tsl], cdfb[:, h, :], ALU.mult)
                            if c < 15:
                                tt("gpsimd", qb[par][:, dc, :], qT[:, dc, tsl], cdfb[:, 4 + h, :], ALU.mult)
                        yb_ = ybs[c % 3]
                        act(yb_, pg, AF.Silu)
                        kf_emit(c + 1)
                    if 0 <= c2 < 16:
                        tt("vector", ygt[p2], yas[p2], ybs[c2 % 3], ALU.mult)
                    if 0 <= c3 < 16:
                        r.c("scalar", lambda e, o_=ygT[:, :, tsl_(c3)], i_=p6[:, 0:512].rearrange("p (a b) -> p a b", a=4):
                            e.copy(out=o_, in_=i_), [p6[:, 0:512]], [ygT[:, :, tsl_(c3)]])
                sp = r.dma("sync", ygs_d.ap()[h], ygT, "ygsp")
                spills.append(sp)
            R2 = Reg(A, RA, RA_BYTES)
            ygg = [R2.alloc([16, 512], BF16) for _ in range(2)]
            wo = R2.alloc([8, 16, 128], BF16)
            ybuf = R2.alloc([8, 512], F32)
            xg = R2.alloc([8, 512], F32)
            r.dma("gpsimd", wo, retwout_d.ap().rearrange("m p (a b) -> p m a b", a=16), "wo")
            for j in range(16):
                ts("vector", wo[:, :, j, :], wo[:, :, j, :], gnwf[:, j:j + 1], None, ALU.mult)
            for g in range(NG):
                gs = slice(g * GT, (g + 1) * GT)
                yb = ygg[g % 2]
                for h in range(4):
                    r.dma("sync", yb[:, h * 4:(h + 1) * 4, :], ygs_d.ap()[h, :, :, gs], "ygg%d_%d" % (g % 2, h),
                          extra_deps=[spills[-4 + h]])
                r.dma("sync", xg, xT_d.ap()[b, :, :, gs], "xg")
                for m in range(8):
                    pb_ = ps(nb())
                    for j in range(16):
                        mm(pb_, wo[:, m, j, :], yb[:, j, :], j == 0, j == 15)
                    act(ybuf[:, m, :], pb_, AF.Copy)
                    act(sqb[:, m, :], pb_, AF.Square)
                postnorm(ybuf, P[:, 2, :], lambda c: xg[:, c, :], lambda c: xT[:, c, gs])

        spills = []

        def mla(b):
            P = prm[1][b]
            R = Reg(A, RA, RA_BYTES)
            Wr = R.alloc([20 * 1024 // 2], BF16)
            Wbase = RA
            win = A.view(Wbase, [8, 704], BF16)
            wkr = A.view(Wbase + 8 * 704 * 2, [8, 64], BF16)
            wuq = A.view(Wbase, [3, 1536], BF16)
            wukv = A.view(Wbase + 9216, [2, 2048], BF16)
            wqr = A.view(Wbase + 9216 + 8192, [3, 512], BF16)
            wout = A.view(Wbase, [8, 1024], BF16)
            Tbase = RA + 20 * 1024
            T = R.alloc([21 * 1024 // 2], BF16)
            hdn = A.view(Tbase, [8, 512], BF16)
            c32 = A.view(Tbase + 8192, [5, 512], F32)
            qTn = A.view(Tbase, [S], BF16)
            qTr = A.view(Tbase + 4096, [S], BF16)
            kTn = A.view(Tbase + 8192, [S], BF16)
            vtm = A.view(Tbase + 12288, [16, 128], BF16)
            pT = [A.view(Tbase + 16384 + i * 1024, [512], BF16) for i in range(5)]
            rden = rs
            ybuf = A.view(Tbase, [8, 512], F32)
            cqn = R.alloc([3, S], BF16)
            ckvn = R.alloc([2, S], BF16)
            krT = R.alloc([S], BF16)
            cos64 = R.alloc([S], F32)
            sin64 = R.alloc([S], F32)
            oT = R.alloc([8, S], BF16)
            rope_prep(b, RA + RA_BYTES - 32768)
            memset("gpsimd", krT[64:128], 0.0)
            r.dma("gpsimd", win, mwin_d.ap().rearrange("p (a b) -> p a b", a=8), "mw0")
            ts("vector", wkr[:, :, 0:32], win[:, :, 672:704], -1.0, None, ALU.mult)
            cp("gpsimd", wkr[:, :, 32:64], win[:, :, 640:672])
            for g in range(NG):
                gs = slice(g * GT, (g + 1) * GT)
                prenorm_a(lambda c: xT[:, c, gs])
                rope_slice(invf_mla, 64, cos64[0:64], sin64[0:64], RA + RA_BYTES - 32768, gs)
                prenorm_b(lambda c: xT[:, c, gs], P[:, 0, :], P[:, 1, :], hdn)
                for j in range(5):
                    pb_ = ps(nb())
                    for kc in range(8):
                        mm(pb_, win[:, kc, j * 128:(j + 1) * 128], hdn[:, kc, :], kc == 0, kc == 7)
                    act(c32[:, j, :], pb_, AF.Copy)
                    act(sqb[:, j, :], pb_, AF.Square)
                pk, pkr = ps(nb()), ps(nb())
                for kc in range(8):
                    mm(pk[0:64], win[:, kc, 640:704], hdn[:, kc, :], kc == 0, kc == 7)
                for kc in range(8):
                    mm(pkr[0:64], wkr[:, kc, :], hdn[:, kc, :], kc == 0, kc == 7)
                tt("vector", scr[0][0:64], pk[0:64], cos64[0:64, gs], ALU.mult)
                tt("vector", scr[1][0:64], pkr[0:64], sin64[0:64, gs], ALU.mult)
                tt("gpsimd", krT[0:64, gs], scr[0][0:64], scr[1][0:64], ALU.add)
                p7 = ps(7)
                for c in range(3):
                    mm(p7, ones, sqb[:, c, :], c == 0, c == 2)
                act(rs, p7, AF.Ln, scale=1.0 / 384, bias=eps6)
                act(rs, rs, AF.Exp, scale=-0.5)
                for c in range(3):
                    stt("vector", cqn[:, c, gs], c32[:, c, :], mnorm[:, c:c + 1], rs, ALU.mult, ALU.mult)
                p7 = ps(7)
                for c in range(2):
                    mm(p7, ones, sqb[:, 3 + c, :], c == 0, c == 1)
                act(rs, p7, AF.Ln, scale=1.0 / 256, bias=eps6)
                act(rs, rs, AF.Exp, scale=-0.5)
                for c in range(2):
                    stt("vector", ckvn[:, c, gs], c32[:, 3 + c, :], mnorm[:, 3 + c:4 + c], rs, ALU.mult, ALU.mult)
            r.dma("gpsimd", wuq, mwuq_d.ap().rearrange("p (a b) -> p a b", a=3), "mw1")
            r.dma("gpsimd", wukv, mwukv_d.ap().rearrange("p (a b) -> p a b", a=2), "mw2")
            for h in range(8):
                c0 = h * 192 + 128
                ts("vector", wqr[:, :, h * 64:h * 64 + 32], wuq[:, :, c0 + 32:c0 + 64], -1.0, None, ALU.mult)
                cp("gpsimd", wqr[:, :, h * 64 + 32:h * 64 + 64], wuq[:, :, c0:c0 + 32])
            scale = 192.0 ** -0.5
            memset("gpsimd", qTr[64:128], 0.0)
            zeros32 = sqb[:, 2:4, :].bitcast(F32).rearrange("p a b -> p (a b)")
            memset("gpsimd", zeros32, 0.0)
            for h in range(8):
                if b == 0 and 0 in layers:
                    d2d[("w1", 1, h)] = r.dma("gpsimd", w1s_d.ap()[1, h], w1r_d.ap()[1, h], "d2dw1_1")
                    d2d[("w2", 1, h)] = r.dma("gpsimd", w2s_d.ap()[1, h], w2r_d.ap()[1, h], "d2dw2_1")
                for g in range(NG):
                    gs = slice(g * GT, (g + 1) * GT)
                    pq = ps(nb())
                    for kc in range(3):
                        mm(pq, wuq[:, kc, h * 192:h * 192 + 128], cqn[:, kc, gs], kc == 0, kc == 2)
                    act(qTn[:, gs], pq, AF.Copy)
                    pk, pkr = ps(nb()), ps(nb())
                    for kc in range(3):
                        mm(pk[0:64], wuq[:, kc, h * 192 + 128:h * 192 + 192], cqn[:, kc, gs], kc == 0, kc == 2)
                    for kc in range(3):
                        mm(pkr[0:64], wqr[:, kc, h * 64:(h + 1) * 64], cqn[:, kc, gs], kc == 0, kc == 2)
                    tt("vector", scr[0][0:64], pk[0:64], cos64[0:64, gs], ALU.mult)
                    tt("vector", scr[1][0:64], pkr[0:64], sin64[0:64, gs], ALU.mult)
                    tt("gpsimd", qTr[0:64, gs], scr[0][0:64], scr[1][0:64], ALU.add)
                    pkn = ps(nb())
                    for kc in range(2):
                        mm(pkn, wukv[:, kc, h * 256:h * 256 + 128], ckvn[:, kc, gs], kc == 0, kc == 1)
                    act(kTn[:, gs], pkn, AF.Copy)
                    pv = ps(nb())
                    for ti in range(4):
                        t = g * 4 + ti
                        for kc in range(2):
                            mm(pv[:, ti * 128:(ti + 1) * 128], ckvn[:, kc, t * 128:(t + 1) * 128],
                               wukv[:, kc, h * 256 + 128:h * 256 + 256], kc == 0, kc == 1)
                    cp("vector", vtm[:, g * 4:(g + 1) * 4, :], pv.rearrange("p (a b) -> p a b", a=4))
                steps = [(g, kt) for g in range(NG) for kt in range(16)]

                def scores(g, kt):
                    gs = slice(g * GT, (g + 1) * GT)
                    ksl = slice(kt * 128, (kt + 1) * 128)
                    pS = ps(nb(0, 3))
                    mm(pS, kTn[:, ksl], qTn[:, gs], True, False)
                    mm(pS, krT[:, ksl], qTr[:, gs], False, True)
                    return pS
                pend = [scores(*steps[0]), scores(*steps[1]), scores(*steps[2])]
                for i, (g, kt) in enumerate(steps):
                    gs = slice(g * GT, (g + 1) * GT)
                    cur = pend.pop(0)
                    if i + 3 < len(steps):
                        pend.append(scores(*steps[i + 3]))
                    po, pd = (ps(4), ps(5)) if g % 2 == 0 else (ps(6), ps(7))
                    pt = pT[i % 5]
                    act(pt, cur, AF.Exp, scale=scale)
                    mm(po, vtm[:, kt, :], pt, kt == 0, kt == 15)
                    acc0 = scr[2]
                    if kt % 3 == 2:
                        mm(pd, ones, pt, kt == 2, False)
                    elif kt == 0:
                        cp("vector", acc0, pt)
                    else:
                        tt("vector", acc0, acc0, pt, ALU.add)
                    if kt == 15:
                        ah = sqb[:, 0, :]
                        cp("vector", ah, acc0)
                        mm(pd, ones, ah, False, True)
                        act(rden, pd, AF.Ln)
                        act(rden, rden, AF.Exp, scale=-1.0)
                        tt("vector", oT[:, h, gs], po, rden, ALU.mult)
            r.dma("gpsimd", wout, mwout_d.ap().rearrange("p (a b) -> p a b", a=8), "mw3")
            for g in range(NG):
                gs = slice(g * GT, (g + 1) * GT)
                for m in range(8):
                    pb_ = ps(nb())
                    for hh in range(8):
                        mm(pb_, wout[:, hh, m * 128:(m + 1) * 128], oT[:, hh, gs], hh == 0, hh == 7)
                    act(ybuf[:, m, :], pb_, AF.Copy)
                    act(sqb[:, m, :], pb_, AF.Square)
                postnorm(ybuf, P[:, 2, :], lambda c: xT[:, c, gs], lambda c: xT[:, c, gs])

        for b in range(nseq):
            if 0 in layers:
                ret = retention(b)
                if dbg == "hdn0":
                    pass
                elif dbg != "xmix0":
                    mlp(0, b)
            else:
                for g in range(NG):
                    gs = slice(g * GT, (g + 1) * GT)
                    r.dma("sync", xT[:, :, gs], xT_d.ap()[b, :, :, gs], "xld%d" % g)
            if 1 in layers and dbg not in ("hdn0", "xmix0", "x0"):
                if b == 0:
                    if 0 not in layers:
                        emit_d2d(0)
                        emit_d2d(1)
                mla(b)
                if dbg != "xmix1":
                    mlp(1, b)
            if dbg == "hdn0":
                for g in range(NG):
                    gs = slice(g * GT, (g + 1) * GT)
                    for c in range(8):
                        cp("vector", scr[c % 4], ret[:, c, gs])
                        r.dma("sync", out_d.ap()[b, :, c, gs], scr[c % 4], "od%d" % (c % 4))
            elif dbg is not None:
                for g in range(NG):
                    gs = slice(g * GT, (g + 1) * GT)
                    r.dma("sync", out_d.ap()[b, :, :, gs], xT[:, :, gs], "ost%d" % g)
        r.emit()
    return nc


def _host_layout(inp):
    f = np.float32
    sh = {}
    ada_w = np.asarray(inp["ada_w"], f)
    sh["adar"] = np.ascontiguousarray(ada_w.reshape(2, 8, 128, 12, 512).transpose(0, 3, 2, 1, 4)).reshape(2, 12, 128, 4096)
    ada_b = np.asarray(inp["ada_b"], f)
    ab = ada_b.reshape(2, 48, 128).transpose(0, 2, 1)
    sh["adab"] = np.ascontiguousarray(np.repeat(ab[:, :, :, None], 2, axis=3)).reshape(2, 128, 96)
    nw = np.asarray(inp["norm_w"], f)
    sh["normw"] = np.ascontiguousarray(nw.reshape(2, 4, 8, 128).transpose(0, 3, 1, 2)).reshape(2, 128, 32)
    wi = np.asarray(inp["ret_w_in"], f)[0]
    q = wi[:, 0:1024].reshape(1024, 4, 256)
    k = wi[:, 1024:2048].reshape(1024, 4, 256)
    v = wi[:, 2048:4096].reshape(1024, 4, 512)
    g = wi[:, 4096:6144].reshape(1024, 4, 512)
    wh = np.concatenate([q, k, v, g], axis=2)
    sh["retwin"] = np.ascontiguousarray(wh.reshape(8, 128, 4, 1536).transpose(2, 1, 0, 3)).reshape(4, 128, 8 * 1536)
    wo = np.asarray(inp["ret_w_out"], f)[0]
    sh["retwout"] = np.ascontiguousarray(wo.reshape(16, 128, 8, 128).transpose(2, 1, 0, 3)).reshape(8, 128, 2048)
    sh["gnw"] = np.ascontiguousarray(np.asarray(inp["ret_gn_w"], f).reshape(16, 128).T)
    sh["declog"] = np.ascontiguousarray(np.concatenate([np.asarray(inp["ret_decay_logit_fwd"], f)[0],
                                                        np.asarray(inp["ret_decay_logit_bwd"], f)[0]]).reshape(1, 8))
    w1 = np.asarray(inp["mlp_w1"], f)
    sh["w1r"] = np.ascontiguousarray(w1.reshape(2, 8, 128, 8, 512).transpose(0, 3, 2, 1, 4)).reshape(2, 8, 128, 4096)
    w2 = np.asarray(inp["mlp_w2"], f)
    sh["w2r"] = np.ascontiguousarray(w2.reshape(2, 32, 128, 8, 128).transpose(0, 3, 2, 1, 4)).reshape(2, 8, 128, 4096)
    mw = np.asarray(inp["mla_w_in"], f)[0]
    sh["mwin"] = np.ascontiguousarray(mw.reshape(8, 128, 704).transpose(1, 0, 2)).reshape(128, 8 * 704)
    mq = np.asarray(inp["mla_w_uq"], f)[0]
    sh["mwuq"] = np.ascontiguousarray(mq.reshape(3, 128, 1536).transpose(1, 0, 2)).reshape(128, 3 * 1536)
    mk = np.asarray(inp["mla_w_ukv"], f)[0]
    sh["mwukv"] = np.ascontiguousarray(mk.reshape(2, 128, 2048).transpose(1, 0, 2)).reshape(128, 2 * 2048)
    mo = np.asarray(inp["mla_w_out"], f)[0]
    sh["mwout"] = np.ascontiguousarray(mo.reshape(8, 128, 1024).transpose(1, 0, 2)).reshape(128, 8 * 1024)
    mn = np.zeros((128, 8), f)
    mn[:, 0:3] = np.asarray(inp["mla_q_norm"], f)[0].reshape(3, 128).T
    mn[:, 3:5] = np.asarray(inp["mla_kv_norm"], f)[0].reshape(2, 128).T
    sh["mnorm"] = mn
    ct = np.zeros((128, 4 * 128 + 8), f)
    p = np.arange(128, dtype=f)[:, None]
    n = np.arange(128, dtype=f)[None, :]
    ct[:, 0:128] = np.maximum(n - p, 0)
    ct[:, 128:256] = np.maximum(p - n, 0)
    ct[:, 256:384] = n + 1
    ct[:, 384:512] = 128 - n
    ct[:, 512] = 127 - p[:, 0]
    ct[:, 513] = p[:, 0]
    ct[:, 514] = (10000.0 ** (-np.arange(0, 256, 2, dtype=f) / 256)).astype(f)
    im = (10000.0 ** (-np.arange(0, 64, 2, dtype=f) / 64)).astype(f)
    ct[0:32, 515] = im
    ct[32:64, 515] = im
    sh["ctab"] = ct
    return sh


def _core_inputs(inp, core, shared):
    f = np.float32
    b0 = 2 * core
    x = np.asarray(inp["x"], f)[b0:b0 + 2]
    xT = np.ascontiguousarray(x.reshape(2, S, 8, 128).transpose(0, 3, 2, 1))
    c = np.asarray(inp["c"], f)[b0:b0 + 2]
    cT = np.ascontiguousarray(c.reshape(2, 8, 128).transpose(2, 1, 0))
    pos = np.ascontiguousarray(np.asarray(inp["positions"])[b0:b0 + 2].astype(np.int32))
    m = dict(shared)
    m["xT"] = xT
    m["cT"] = cT
    m["pos"] = pos
    return m


_NC_CACHE = {}


def kernel(**inputs):
    shared = _host_layout(inputs)
    if "nc" not in _NC_CACHE:
        _NC_CACHE["nc"] = build()
    nc = _NC_CACHE["nc"]
    in_maps = [_core_inputs(inputs, i, shared) for i in range(8)]
    res = run_bass_kernel_spmd(nc, in_maps, core_ids=list(range(8)))
    out = np.empty((16, S, D), np.float32)
    for i in range(8):
        oT = np.asarray(res.results[i]["outT"])
        out[2 * i:2 * i + 2] = oT.transpose(0, 3, 2, 1).reshape(2, S, D)
    return out
```

```python
import contextlib
import math
import numpy as np
import concourse.bass as bass
import concourse.mybir as mybir
from concourse.bass_utils import run_bass_kernel_spmd

DT = mybir.dt
F32 = DT.float32
BF16 = DT.bfloat16
I32 = DT.int32
AF = mybir.ActivationFunctionType
ALU = mybir.AluOpType

_ESZ = {F32: 4, BF16: 2, I32: 4}
BLK = 64
S = 2048
D = 1024
NG = 4
GT = 512


def esize(dtype):
    return _ESZ[dtype]


def footprint(ap):
    space = str(ap.space)
    if space not in ("SB", "PSUM"):
        return None
    es = esize(ap.dtype)
    dims = ap.ap
    pstep = dims[0][0]
    off = ap.offset
    foff = (off % pstep) * es if pstep > 0 else off * es
    free = [(s * es, c) for (s, c) in dims[1:]]
    if not free:
        ivs = [(foff, foff + es)]
    else:
        inner_s, inner_c = free[-1]
        run = (inner_c - 1) * abs(inner_s) + es
        outer = free[:-1]
        n_outer = 1
        for s, c in outer:
            n_outer *= c
        if n_outer > 256:
            lo = foff
            hi = foff + es
            for s, c in free:
                if s >= 0:
                    hi += s * (c - 1)
                else:
                    lo += s * (c - 1)
            ivs = [(lo, hi)]
        else:
            starts = [foff]
            for s, c in outer:
                starts = [st + s * i for st in starts for i in range(c)]
            base = 0 if inner_s >= 0 else inner_s * (inner_c - 1)
            ivs = [(st + base, st + base + run) for st in starts]
    blk = 2048 if space == "PSUM" else BLK
    blocks = set()
    for lo, hi in ivs:
        blocks.update(range(lo // blk, (hi - 1) // blk + 1))
    return (space, blocks)


class Op:
    __slots__ = ("eng", "fn", "idx", "deps", "signals", "val", "kind", "semkey")


class Rec:
    ENGINES = ("tensor", "scalar", "vector", "gpsimd", "sync")

    def __init__(self, nc):
        self.nc = nc
        self.ops = {e: [] for e in self.ENGINES}
        self.last_w = {}
        self.readers = {}
        self.dma_count = {}
        self.waited = {e: {} for e in self.ENGINES}

    def _add(self, eng, fn, reads, writes, kind="c", semkey=None, extra_deps=()):
        op = Op()
        op.eng, op.fn, op.kind, op.semkey = eng, fn, kind, semkey
        op.signals = False
        op.val = None
        op.idx = len(self.ops[eng])
        rfp = [f for f in (footprint(a) for a in reads) if f is not None]
        wfp = [f for f in (footprint(a) for a in writes) if f is not None]
        wfp = wfp + [f for f in rfp if f[0] == "PSUM"]
        deps = []
        rawset = set()
        for space, blocks in rfp:
            for b in blocks:
                lw = self.last_w.get((space, b))
                if lw is not None:
                    deps.append(lw)
                    rawset.add(id(lw))
        for space, blocks in wfp:
            for b in blocks:
                k = (space, b)
                lw = self.last_w.get(k)
                if lw is not None:
                    deps.append(lw)
                rd = self.readers.get(k)
                if rd:
                    deps.extend(rd.values())
        for p in extra_deps:
            deps.append(p)
            rawset.add(id(p))
        best = {}
        for p in deps:
            if p is op:
                continue
            if p.kind == "c":
                if p.eng == eng:
                    if eng == "tensor":
                        continue
                stream = ("c", p.eng)
                cur = best.get(stream)
                if cur is None or p.idx > cur.idx:
                    best[stream] = p
            else:
                stream = ("d", p.semkey)
                cur = best.get(stream)
                if cur is None or p.val > cur.val:
                    best[stream] = p
        final = []
        w = self.waited[eng]
        for stream, p in best.items():
            key = p.idx if p.kind == "c" else p.val
            if w.get(stream, -1) >= key:
                continue
            w[stream] = key
            p.signals = True
            final.append(p)
        op.deps = final
        if kind == "d":
            self.dma_count[semkey] = self.dma_count.get(semkey, 0) + 16
            op.val = self.dma_count[semkey]
        for space, blocks in rfp:
            for b in blocks:
                self.readers.setdefault((space, b), {})[(eng, kind, semkey)] = op
        for space, blocks in wfp:
            for b in blocks:
                k = (space, b)
                self.last_w[k] = op
                if k in self.readers:
                    self.readers[k] = {}
        self.ops[eng].append(op)
        return op

    def c(self, eng, fn, reads, writes, **kw):
        return self._add(eng, fn, reads, writes, kind="c", **kw)

    def dma(self, eng, out, in_, semkey, **kw):
        def fn(e, out=out, in_=in_):
            return e.dma_start(out=out, in_=in_)
        return self._add(eng, fn, [in_], [out], kind="d", semkey=semkey, **kw)

    def emit(self):
        nc = self.nc
        for e in self.ENGINES:
            cnt = 0
            for op in self.ops[e]:
                if op.kind == "c" and op.signals:
                    cnt += 1
                    op.val = cnt
        sems = {}
        with contextlib.ExitStack() as st:
            for e in self.ENGINES:
                sems[("c", e)] = st.enter_context(nc.semaphore("s_" + e))
            for k in self.dma_count:
                sems[("d", k)] = st.enter_context(nc.semaphore("d_" + str(k)))
            block = st.enter_context(nc.Block())

            def make(e):
                def body(eng):
                    for op in self.ops[e]:
                        for p in op.deps:
                            stream = ("c", p.eng) if p.kind == "c" else ("d", p.semkey)
                            eng.wait_ge(sems[stream], p.val)
                        ins = op.fn(eng)
                        if op.kind == "d":
                            ins.then_inc(sems[("d", op.semkey)], 16)
                        elif op.signals:
                            ins.then_inc(sems[("c", e)], 1)
                    if e == "sync":
                        for k, v in self.dma_count.items():
                            eng.wait_ge(sems[("d", k)], v)
                return body
            for e in self.ENGINES:
                getattr(block, e)(make(e))


class Arena:
    def __init__(self, handle, nbytes):
        self.h = handle
        self.nbytes = nbytes
        self.top = 0

    def view(self, off, shape, dtype):
        n = 1
        for s in shape:
            n *= s
        nb = n * esize(dtype)
        assert off % 4 == 0 and off + nb <= self.nbytes, (off, nb, self.nbytes)
        ap = self.h[:, off // 4:(off + nb + 3) // 4]
        if dtype != F32:
            ap = ap.bitcast(dtype)
        if len(shape) == 2:
            ap = ap.rearrange("p (a b) -> p a b", a=shape[0])
        elif len(shape) == 3:
            ap = ap.rearrange("p (a b c) -> p a b c", a=shape[0], b=shape[1])
        return ap

    def alloc(self, shape, dtype):
        n = 1
        for s in shape:
            n *= s
        nb = (n * esize(dtype) + 63) // 64 * 64
        off = self.top
        self.top += nb
        return self.view(off, shape, dtype)


class Reg:
    def __init__(self, arena, base, size):
        self.a, self.base, self.size, self.top = arena, base, size, 0

    def alloc(self, shape, dtype):
        n = 1
        for s in shape:
            n *= s
        nb = (n * esize(dtype) + 63) // 64 * 64
        assert self.top + nb <= self.size, ("region overflow", self.top, nb, self.size)
        v = self.a.view(self.base + self.top, shape, dtype)
        self.top += nb
        return v


ARENA_BYTES = 211968
RX_BYTES = 64 * 1024
RA_BYTES = 113 * 1024


def build(layers=(0, 1), nseq=2, dbg=None):
    nc = bass.Bass("TRN2", target_bir_lowering=False)

    def din(name, shape, dtype=F32):
        return nc.dram_tensor(name, list(shape), dtype, kind="ExternalInput")
    xT_d = din("xT", [2, 128, 8, S])
    cT_d = din("cT", [128, 8, 2])
    pos_d = din("pos", [2, S], I32)
    ctab_d = din("ctab", [128, 4 * 128 + 8])
    adar_d = din("adar", [2, 12, 128, 8 * 512])
    adab_d = din("adab", [2, 128, 48 * 2])
    normw_d = din("normw", [2, 128, 32])
    retwin_d = din("retwin", [4, 128, 8 * 1536])
    retwout_d = din("retwout", [8, 128, 16 * 128])
    gnw_d = din("gnw", [128, 16])
    declog_d = din("declog", [1, 8])
    w1r_d = din("w1r", [2, 8, 128, 8 * 512])
    w2r_d = din("w2r", [2, 8, 128, 32 * 128])
    mwin_d = din("mwin", [128, 8 * 704])
    mwuq_d = din("mwuq", [128, 3 * 1536])
    mwukv_d = din("mwukv", [128, 2 * 2048])
    mwout_d = din("mwout", [128, 8 * 1024])
    mnorm_d = din("mnorm", [128, 8])
    out_d = nc.dram_tensor("outT", [2, 128, 8, S], F32, kind="ExternalOutput")
    w1s_d = nc.dram_tensor("w1s", [2, 8, 128, 8 * 512], BF16, kind="Internal")
    w2s_d = nc.dram_tensor("w2s", [2, 8, 128, 32 * 128], BF16, kind="Internal")
    ygs_d = nc.dram_tensor("ygs", [4, 128, 4, S], BF16, kind="Internal")

    with contextlib.ExitStack() as st:
        ah = st.enter_context(nc.sbuf_tensor("arena", [128, ARENA_BYTES // 4], F32))
        ph = st.enter_context(nc.psum_tensor("psum", [128, 4096], F32))
        A = Arena(ah, ARENA_BYTES)
        r = Rec(nc)

        def ps(b, n=512):
            return ph[:, b * 512:b * 512 + n]

        def psb(b):
            return ph[:, b * 512:(b + 1) * 512].bitcast(BF16)

        def mm(out, lhsT, rhs, start, stop):
            r.c("tensor", lambda e: e.matmul(out, lhsT=lhsT, rhs=rhs, start=start, stop=stop), [lhsT, rhs], [out])

        def tr(out, in_, ident):
            r.c("tensor", lambda e: e.transpose(out, in_, ident), [in_, ident], [out])

        def act(out, in_, func, scale=1.0, bias=None, accum=None, eng="scalar"):
            rd = [in_]
            kw = {}
            if bias is not None:
                kw["bias"] = bias
                rd.append(bias)
            if not isinstance(scale, (int, float)):
                rd.append(scale)
            wr = [out]
            if accum is not None:
                kw["accum_out"] = accum
                wr.append(accum)
            r.c("scalar", lambda e: e.activation(out=out, in_=in_, func=func, scale=scale, **kw), rd, wr)

        def tt(eng, out, in0, in1, op):
            r.c(eng, lambda e: e.tensor_tensor(out=out, in0=in0, in1=in1, op=op), [in0, in1], [out])

        def ts(eng, out, in0, s1, s2, op0, op1=None):
            rd = [in0] + [s for s in (s1, s2) if s is not None and not isinstance(s, (int, float))]
            if op1 is None:
                r.c(eng, lambda e: e.tensor_scalar(out=out, in0=in0, scalar1=s1, scalar2=None, op0=op0), rd, [out])
            else:
                r.c(eng, lambda e: e.tensor_scalar(out=out, in0=in0, scalar1=s1, scalar2=s2, op0=op0, op1=op1), rd, [out])

        def stt(eng, out, in0, sc, in1, op0, op1):
            rd = [in0, in1] + ([] if isinstance(sc, (int, float)) else [sc])
            r.c(eng, lambda e: e.scalar_tensor_tensor(out=out, in0=in0, scalar=sc, in1=in1, op0=op0, op1=op1), rd, [out])

        def cp(eng, out, in_):
            if eng == "scalar":
                r.c(eng, lambda e: e.copy(out=out, in_=in_), [in_], [out])
            else:
                r.c(eng, lambda e: e.tensor_copy(out=out, in_=in_), [in_], [out])

        def recip(out, in_):
            r.c("vector", lambda e: e.reciprocal(out=out, in_=in_), [in_], [out])

        def memset(eng, out, v):
            r.c(eng, lambda e: e.memset(out, v), [], [out])

        ones = A.alloc([128], BF16)
        ident = A.alloc([128], BF16)
        ctab = A.alloc([4 * 128 + 8], F32)
        eps6 = A.alloc([1], F32)
        eps5 = A.alloc([1], F32)
        negpi = A.alloc([1], F32)
        cst = A.alloc([2], F32)
        cact = A.alloc([8, 2], BF16)
        cin = A.alloc([8, 2], F32)
        modt = [A.alloc([48, 2], F32) for _ in range(2)]
        adab = [A.alloc([48, 2], F32) for _ in range(2)]
        normw = [A.alloc([4, 8], F32) for _ in range(2)]
        prm = [[A.alloc([6, 8], F32) for _ in range(2)] for _ in range(2)]
        lg = A.alloc([8], F32)
        lgt = A.alloc([8], F32)
        Dm = A.alloc([4, 128], F32)
        cdfb = A.alloc([8, 128], F32)
        sd = A.alloc([8], F32)
        cdec = A.alloc([8], F32)
        mnorm = A.alloc([8], F32)
        gnwf = A.alloc([16], F32)
        stat = A.alloc([32], F32)
        scr = [A.alloc([512], F32) for _ in range(4)]
        rs = A.alloc([512], F32)
        sqb = A.alloc([8, 512], BF16)
        misc_end = A.top
        RX = misc_end
        RA = RX + RX_BYTES
        assert RA + RA_BYTES <= ARENA_BYTES, (RA + RA_BYTES, ARENA_BYTES)
        xT = A.view(RX, [8, S], F32)

        diffpos = ctab[:, 0:128]
        diffneg = ctab[:, 128:256]
        np1 = ctab[:, 256:384]
        n128m = ctab[:, 384:512]
        c127mp = ctab[:, 512:513]
        cp_ = ctab[:, 513:514]
        invf_ret = ctab[:, 514:515]
        invf_mla = ctab[:, 515:516]
        identf = None

        memset("vector", ones, 1.0)
        memset("vector", eps6, 1e-6)
        memset("vector", eps5, 1e-5)
        memset("vector", negpi, -math.pi)
        memset("vector", cst[:, 0:1], -0.5)
        memset("vector", cst[:, 1:2], -1.0)
        r.dma("sync", ctab, ctab_d.ap(), "c0")
        r.dma("sync", cin, cT_d.ap().rearrange("p (a b) -> p a b", a=8) if False else cT_d.ap(), "c1")
        r.dma("sync", mnorm, mnorm_d.ap(), "c2")
        r.dma("sync", gnwf, gnw_d.ap(), "c6")
        for l in range(2):
            r.dma("sync", adab[l], adab_d.ap()[l].rearrange("p (a b) -> p a b", a=48), "c3_%d" % l)
            r.dma("sync", normw[l], normw_d.ap()[l].rearrange("p (a b) -> p a b", a=4), "c4_%d" % l)
        r.dma("sync", lgt, declog_d.ap().partition_broadcast(128), "c5")
        tt("vector", scr[0][:, 0:128], diffpos, diffneg, ALU.add)
        ts("vector", ident, scr[0][:, 0:128], 0.0, None, ALU.is_equal)
        act(cact, cin, AF.Silu)

        d2d = {}

        def emit_d2d(l):
            for s_ in range(8):
                d2d[("w1", l, s_)] = r.dma("gpsimd", w1s_d.ap()[l, s_], w1r_d.ap()[l, s_], "d2dw1_%d" % l)
                d2d[("w2", l, s_)] = r.dma("gpsimd", w2s_d.ap()[l, s_], w2r_d.ap()[l, s_], "d2dw2_%d" % l)

        ai_ = [0]

        def emit_ada(l):
            RAr = Reg(A, RA, RA_BYTES)
            adabuf = [RAr.alloc([8, 512], BF16) for _ in range(2)]
            for s_ in range(12):
                buf = adabuf[ai_[0] % 2]
                r.dma("gpsimd", buf, adar_d.ap()[l, s_].rearrange("p (a b) -> p a b", a=8), "ada%d" % (ai_[0] % 2))
                ai_[0] += 1
                pm = ps(7, 8)
                for j4 in range(4):
                    for kc in range(8):
                        mm(pm[:, j4 * 2:j4 * 2 + 2], buf[:, kc, j4 * 128:(j4 + 1) * 128], cact[:, kc, :], kc == 0, kc == 7)
                tt("vector", modt[l][:, s_ * 4:(s_ + 1) * 4, :], pm.rearrange("p (a b) -> p a b", a=4),
                   adab[l][:, s_ * 4:(s_ + 1) * 4, :], ALU.add)
            ada_params(l)

        def ada_small(l, j, base):
            buf = A.view(base + (j % 2) * 2048, [8, 128], BF16)
            s_, j4 = j // 4, j % 4
            r.dma("gpsimd", buf, adar_d.ap()[l, s_].rearrange("p (a b) -> p a b", a=8)[:, :, j4 * 128:(j4 + 1) * 128],
                  "adas%d" % (j % 2))
            pm = ps(7, 2)
            for kc in range(8):
                mm(pm, buf[:, kc, :], cact[:, kc, :], kc == 0, kc == 7)
            tt("vector", modt[l][:, j, :], pm, adab[l][:, j, :], ALU.add)

        def ada_params(l):
            for b in range(2):
                P = prm[l][b]
                m = modt[l]
                stt("vector", P[:, 0, :], m[:, 8:16, b], 1.0, normw[l][:, 0, :], ALU.add, ALU.mult)
                cp("vector", P[:, 1, :], m[:, 0:8, b])
                tt("vector", P[:, 2, :], m[:, 16:24, b], normw[l][:, 1, :], ALU.mult)
                stt("vector", P[:, 3, :], m[:, 32:40, b], 1.0, normw[l][:, 2, :], ALU.add, ALU.mult)
                cp("vector", P[:, 4, :], m[:, 24:32, b])
                tt("vector", P[:, 5, :], m[:, 40:48, b], normw[l][:, 3, :], ALU.mult)

        hoisted_rope = [False]
        if 0 in layers:
            pass

        if 0 in layers:
            act(lg, lgt, AF.Exp, scale=-1.0)
            ts("vector", lg, lg, 1.0, None, ALU.add)
            act(lg, lg, AF.Ln)
            ts("vector", lg, lg, -1.0, None, ALU.mult)
            for h in range(4):
                t1 = scr[0][:, 0:128]
                ts("vector", t1, diffpos, lg[:, h:h + 1], None, ALU.mult)
                stt("vector", t1, diffneg, lg[:, 4 + h:5 + h], t1, ALU.mult, ALU.add)
                act(Dm[:, h, :], t1, AF.Exp)
                act(cdfb[:, h, :], np1, AF.Exp, scale=lg[:, h:h + 1])
                act(cdfb[:, 4 + h, :], n128m, AF.Exp, scale=lg[:, 4 + h:5 + h])
                act(sd[:, h:h + 1], c127mp, AF.Exp, scale=lg[:, h:h + 1])
                act(sd[:, 4 + h:5 + h], cp_, AF.Exp, scale=lg[:, 4 + h:5 + h])
            act(cdec, lg, AF.Exp, scale=128.0)

        bank_rot = [0]

        def nb(lo=0, hi=5):
            b = lo + bank_rot[0] % (hi - lo + 1)
            bank_rot[0] += 1
            return b

        def rstd_from_sq(nch, dim, sq=None):
            sq = sqb if sq is None else sq
            p7 = ps(7)
            for c in range(nch):
                mm(p7, ones, sq[:, c, :], c == 0, c == nch - 1)
            act(rs, p7, AF.Ln, scale=1.0 / dim, bias=eps6)
            act(rs, rs, AF.Exp, scale=-0.5)

        def prenorm_a(src):
            for c in range(8):
                act(sqb[:, c, :], src(c), AF.Square)
            rstd_from_sq(8, D)

        def prenorm_b(src, Aa, Bb, hdn):
            for c in range(8):
                t = scr[c % 2]
                tt("vector", t, src(c), rs, ALU.mult)
                act(hdn[:, c, :], t, AF.Identity, scale=Aa[:, c:c + 1], bias=Bb[:, c:c + 1])

        def prenorm(src, Aa, Bb, hdn):
            prenorm_a(src)
            prenorm_b(src, Aa, Bb, hdn)

        def postnorm(ybuf, G, xin, xout, sq=None):
            rstd_from_sq(8, D, sq)
            for c in range(8):
                t = scr[c % 2]
                tt("vector", t, ybuf[:, c, :], rs, ALU.mult)
                stt("vector", xout(c), t, G[:, c:c + 1], xin(c), ALU.mult, ALU.add)

        def rope_prep(b, base):
            ti = A.view(base, [S], I32)
            r.dma("sync", ti, pos_d.ap()[b:b + 1, :].partition_broadcast(128), "pos")

        def rope_slice(invf, npart, cosT, sinT, base, gs):
            P_ = slice(0, npart)
            ti = A.view(base, [S], I32)[P_, gs]
            tg = A.view(base, [S], F32)[P_, gs]
            ua = A.view(base + 8192, [S], F32)[P_, gs]
            tf = A.view(base + 16384, [S], F32)[P_, gs]
            cp("vector", ua, ti)
            ts("vector", ua, ua, invf[P_], 1.0 / (2 * math.pi), ALU.mult, ALU.mult)
            for shift, dst in ((0.0, sinT), (0.25, cosT)):
                ts("vector", tf, ua, shift, None, ALU.add)
                cp("vector", ti, tf)
                cp("vector", tg, ti)
                tt("vector", tf, tf, tg, ALU.subtract)
                ts("vector", tg, tf, 0.5, None, ALU.is_gt)
                tt("vector", tf, tf, tg, ALU.subtract)
                ts("vector", tg, tf, -0.5, None, ALU.is_lt)
                tt("vector", tf, tf, tg, ALU.add)
                act(dst[:, gs], tf, AF.Sin, scale=2 * math.pi)

        def rope_tables(b, invf, npart, cosT, sinT, base):
            rope_prep(b, base)
            for g_ in range(NG):
                rope_slice(invf, npart, cosT, sinT, base, slice(g_ * GT, (g_ + 1) * GT))

        if 0 in layers:
            rope_tables(0, invf_ret, 128, A.view(RX + 32768, [S], F32), A.view(RX + 40960, [S], F32), RA + 32768)
            hoisted_rope[0] = True
        emit_ada(layers[0])

        def mlp(l, b):
            P = prm[l][b]
            R = Reg(A, RA, RA_BYTES)
            hT = R.alloc([32, 512], BF16)
            ring_base = RA + R.top
            ring = [R.alloc([4096], BF16) for _ in range(4)]
            ring_i = [0]

            def next_slab(shape):
                k = ring_i[0] % 4
                ring_i[0] += 1
                return A.view(ring_base + k * 8192, shape, BF16), k
            ybuf = R.alloc([8, 512], F32)
            hdns = [R.alloc([8, 512], BF16) for _ in range(2)]
            ysq = R.alloc([8, 512], BF16)

            def xsrc(g):
                gs = slice(g * GT, (g + 1) * GT)
                return lambda c: xT[:, c, gs]
            prenorm(xsrc(0), P[:, 3, :], P[:, 4, :], hdns[0])
            for g in range(NG):
                gs = slice(g * GT, (g + 1) * GT)
                hdn = hdns[g % 2]
                for s in range(8):
                    wb, k_ = next_slab([8, 512])
                    r.dma("sync", wb, w1s_d.ap()[l, s].rearrange("p (a b) -> p a b", a=8), "wring%d" % k_,
                          extra_deps=[d2d[("w1", l, 7)]])
                    for j4 in range(4):
                        j = s * 4 + j4
                        pb = ps(nb())
                        for kc in range(8):
                            mm(pb, wb[:, kc, j4 * 128:(j4 + 1) * 128], hdn[:, kc, :], kc == 0, kc == 7)
                        t = scr[2 + j % 2]
                        act(t, pb, AF.Relu)
                        tt("vector", hT[:, j, :], pb, t, ALU.mult)
                    if s == 1 and g + 1 < NG:
                        prenorm_a(xsrc(g + 1))
                    if l == 0 and b == 0 and 1 in layers:
                        p_ = g * 8 + s
                        for j in range(p_ * 3 // 2, (p_ + 1) * 3 // 2):
                            ada_small(1, j, RA + R.top)
                        if p_ == 31:
                            ada_params(1)
                if g + 1 < NG:
                    prenorm_b(xsrc(g + 1), P[:, 3, :], P[:, 4, :], hdns[(g + 1) % 2])
                for m in range(8):
                    wb, k_ = next_slab([32, 128])
                    r.dma("sync", wb, w2s_d.ap()[l, m].rearrange("p (a b) -> p a b", a=32), "wring%d" % k_,
                          extra_deps=[d2d[("w2", l, 7)]])
                    pb = ps(nb())
                    for j in range(32):
                        mm(pb, wb[:, j, :], hT[:, j, :], j == 0, j == 31)
                    act(ybuf[:, m, :], pb, AF.Copy)
                    act(ysq[:, m, :], pb, AF.Square)
                postnorm(ybuf, P[:, 5, :], lambda c: xT[:, c, gs], lambda c: xT[:, c, gs], sq=ysq)
                if l == layers[-1] and dbg is None:
                    r.dma("gpsimd", out_d.ap()[b, :, :, gs], xT[:, :, gs], "ost%d" % g)

        def retention(b):
            P = prm[0][b]
            RXr = Reg(A, RX, RX_BYTES)
            hdnT = RXr.alloc([8, S], BF16)
            cosT = RXr.alloc([S], F32)
            sinT = RXr.alloc([S], F32)
            xs = [RXr.alloc([8, 256], F32) for _ in range(2)]
            R = Reg(A, RA, RA_BYTES)
            wh = R.alloc([8, 1024], BF16)
            wg = R.alloc([8, 512], BF16)
            qT = R.alloc([2, S], BF16)
            kT = R.alloc([2, S], BF16)
            ktm = R.alloc([16, 256], BF16)
            vtm = R.alloc([16, 512], BF16)
            Sbs = R.alloc([16, 2, 512], BF16)
            S32 = R.alloc([2, 512], F32)
            Sb32 = S32
            Sfb = [R.alloc([2, 512], BF16) for _ in range(2)]
            sT = [R.alloc([128], BF16) for _ in range(2)]
            qf = [R.alloc([2, 128], BF16) for _ in range(2)]
            qb = [R.alloc([2, 128], BF16) for _ in range(2)]
            kf = [R.alloc([256], BF16) for _ in range(3)]
            ygt = [R.alloc([512], BF16) for _ in range(2)]
            ygT = A.view(RX + 48 * 1024, [4, S], BF16)
            rope_inline = not (b == 0 and hoisted_rope[0])
            if rope_inline:
                rope_prep(b, RA + 32768)
            for g in range(NG):
                for c in range(8):
                    pass
                for hf in range(2):
                    t0 = g * GT + hf * 256
                    r.dma("sync", xs[hf], xT_d.ap()[b, :, :, t0:t0 + 256], "xs%d" % hf)
                    for c in range(8):
                        act(sqb[:, c, hf * 256:(hf + 1) * 256], xs[hf][:, c, :], AF.Square)
                rstd_from_sq(8, D)
                if rope_inline:
                    rope_slice(invf_ret, 128, cosT, sinT, RA + 32768, slice(g * GT, (g + 1) * GT))
                for hf in range(2):
                    t0 = g * GT + hf * 256
                    for c in range(8):
                        t = scr[c % 2][:, 0:256]
                        tt("vector", t, xs[hf][:, c, :], rs[:, hf * 256:(hf + 1) * 256], ALU.mult)
                        act(hdnT[:, c, t0:t0 + 256], t, AF.Identity, scale=P[:, 0, c:c + 1], bias=P[:, 1, c:c + 1])
            if dbg == "hdn0":
                return hdnT
            def emit_wh(h_):
                r.dma("gpsimd", wh, retwin_d.ap()[h_].rearrange("p (a b) -> p a b", a=8)[:, :, 0:1024], "wh")

            emit_wh(0)
            for h in range(4):
                r.dma("gpsimd", wg, retwin_d.ap()[h].rearrange("p (a b) -> p a b", a=8)[:, :, 1024:1536], "wg")
                memset("gpsimd", Sb32, 0.0)
                memset("gpsimd", Sbs[:, 15, :, :], 0.0)

                def kfb_emit(c):
                    if c >= 1:
                        ts("vector", kf[c % 3], ktm[:, c, :], sd[:, 4 + h:5 + h], None, ALU.mult)

                def bstep(c, nxt):
                    kfb = kf[c % 3]
                    pbs = (ps(nb()), ps(nb()))
                    for dc in range(2):
                        mm(pbs[dc], kfb[:, dc * 128:(dc + 1) * 128], vtm[:, c, :], True, True)
                    if nxt is not None:
                        kfb_emit(nxt)
                    for dc in range(2):
                        stt("vector", Sb32[:, dc, :], Sb32[:, dc, :], cdec[:, 4 + h:5 + h], pbs[dc], ALU.mult, ALU.add)
                    cp("scalar", Sbs[:, c - 1, :, :], Sb32)

                for g in (3, 2, 1, 0):
                    gs = slice(g * GT, (g + 1) * GT)
                    for which, dst, kscale in ((0, qT, 1.0), (1, kT, 1.0 / 16.0)):
                        pa, pbk = ps(nb()), ps(nb())
                        for dc, pp in ((0, pa), (1, pbk)):
                            col = which * 256 + dc * 128
                            for kc in range(8):
                                mm(pp, wh[:, kc, col:col + 128], hdnT[:, kc, gs], kc == 0, kc == 7)
                        stt("vector", scr[0], pa, kscale, cosT[:, gs], ALU.mult, ALU.mult)
                        stt("vector", scr[1], pbk, kscale, sinT[:, gs], ALU.mult, ALU.mult)
                        tt("gpsimd", dst[:, 0, gs], scr[0], scr[1], ALU.subtract)
                        stt("vector", scr[2], pa, kscale, sinT[:, gs], ALU.mult, ALU.mult)
                        stt("vector", scr[3], pbk, kscale, cosT[:, gs], ALU.mult, ALU.mult)
                        tt("gpsimd", dst[:, 1, gs], scr[2], scr[3], ALU.add)
                    for ti in range(4):
                        t = g * 4 + ti
                        tsl = slice(t * 128, (t + 1) * 128)
                        pv = ps(nb())
                        for kc in range(8):
                            mm(pv, hdnT[:, kc, tsl], wh[:, kc, 512:1024], kc == 0, kc == 7)
                        act(vtm[:, t, :], pv, AF.Copy)
                        p6 = psb(6)
                        for dc in range(2):
                            tr(p6[:, dc * 128:(dc + 1) * 128], kT[:, dc, tsl], ident)
                        cp("vector", ktm[:, t, :], p6[:, 0:256])
                    cs = [c for c in range(4 * g + 3, 4 * g - 1, -1) if c >= 1]
                    kfb_emit(cs[0])
                    if len(cs) > 1:
                        kfb_emit(cs[1])
                    for i_, c in enumerate(cs):
                        bstep(c, cs[i_ + 2] if i_ + 2 < len(cs) else None)
                if h + 1 < 4:
                    emit_wh(h + 1)
                if b == 0:
                    for s_ in (2 * h, 2 * h + 1):
                        d2d[("w1", 0, s_)] = r.dma("gpsimd", w1s_d.ap()[0, s_], w1r_d.ap()[0, s_], "d2dw1_0")
                        d2d[("w2", 0, s_)] = r.dma("gpsimd", w2s_d.ap()[0, s_], w2r_d.ap()[0, s_], "d2dw2_0")
                memset("gpsimd", S32, 0.0)
                ybs = [scr[0], scr[1], scr[2]]
                yas = [scr[3], rs]
                pos_ = {}

                def tsl_(c):
                    return slice(c * 128, (c + 1) * 128)

                def kf_emit(c):
                    if c < 15:
                        ts("vector", kf[c % 2], ktm[:, c, :], sd[:, h:h + 1], None, ALU.mult)

                kf_emit(0)
                for i in range(16 + 3):
                    c = i
                    c1, c2, c3 = i - 1, i - 2, i - 3
                    if 0 <= c2 < 16:
                        p2 = c2 % 2
                        po2 = pos_[c2]
                        st_ = stat[:, p2 * 16:(p2 + 1) * 16]
                        r.c("vector", lambda e, po=po2, st_=st_: e.bn_stats(out=st_[:, 0:6], in_=po), [po2], [st_[:, 0:6]])
                        r.c("vector", lambda e, st_=st_: e.bn_aggr(out=st_[:, 6:8], in_=st_[:, 0:6]), [st_[:, 0:6]], [st_[:, 6:8]])
                        tt("gpsimd", st_[:, 8:9], st_[:, 7:8], eps5, ALU.add)
                        tt("gpsimd", st_[:, 9:10], st_[:, 8:9], cst[:, 0:1], ALU.pow)
                        tt("gpsimd", st_[:, 11:12], st_[:, 6:7], cst[:, 1:2], ALU.mult)
                        tt("gpsimd", st_[:, 10:11], st_[:, 11:12], st_[:, 9:10], ALU.mult)
                    if c < 16:
                        par = c % 2
                        tsl = tsl_(c)
                        if c < 15:
                            for dc in range(2):
                                mm(ps(dc), kf[par][:, dc * 128:(dc + 1) * 128], vtm[:, c, :], True, True)
                        pS = ps(2)[:, 0:128]
                        for dc in range(2):
                            mm(pS, kT[:, dc, tsl], qT[:, dc, tsl], dc == 0, dc == 1)
                        pg = ps(3)
                        for kc in range(8):
                            mm(pg, hdnT[:, kc, tsl], wg[:, kc, :], kc == 0, kc == 7)
                    if 0 <= c1 < 16:
                        p1 = c1 % 2
                        po = ps((4, 5, 7)[c1 % 3])
                        pos_[c1] = po
                        ops_ = [(sT[p1], vtm[:, c1, :])]
                        if c1 > 0:
                            ops_ += [(qf[p1][:, 0, :], Sfb[p1][:, 0, :]), (qf[p1][:, 1, :], Sfb[p1][:, 1, :])]
                        if c1 < 15:
                            ops_ += [(qb[p1][:, 0, :], Sbs[:, c1, 0, :]), (qb[p1][:, 1, :], Sbs[:, c1, 1, :])]
                        for k_, (lt, rh) in enumerate(ops_):
                            mm(po, lt, rh, k_ == 0, k_ == len(ops_) - 1)
                    if 0 <= c3 < 16:
                        p6 = psb(6)
                        for j in range(4):
                            tr(p6[:, j * 128:(j + 1) * 128], ygt[c3 % 2][:, j * 128:(j + 1) * 128], ident)
                    if 0 <= c2 < 16:
                        act(yas[p2], po2, AF.Identity, scale=st_[:, 9:10], bias=st_[:, 10:11])
                    if c < 16:
                        if c < 15:
                            for dc in range(2):
                                stt("vector", S32[:, dc, :], S32[:, dc, :], cdec[:, h:h + 1], ps(dc), ALU.mult, ALU.add)
                            cp("scalar", Sfb[1 - par], S32)
                        tt("vector", sT[par], pS, Dm[:, h, :], ALU.mult)
                        for dc in range(2):
                            if c > 0:
                                tt("gpsimd", qf[par][:, dc, :], qT[:, dc, tsl], cdfb[:, h, :], ALU.mult)
                            if c < 15:
                                tt("gpsimd", qb[par][:, dc, :], qT[:, dc, tsl], cdfb[:, 4 + h, :], ALU.mult)
                        yb_ = ybs[c % 3]
                        act(yb_, pg, AF.Silu)
                        kf_emit(c + 1)
                    if 0 <= c2 < 16:
                        tt("vector", ygt[p2], yas[p2], ybs[c2 % 3], ALU.mult)
                    if 0 <= c3 < 16:
                        r.c("scalar", lambda e, o_=ygT[:, :, tsl_(c3)], i_=p6[:, 0:512].rearrange("p (a b) -> p a b", a=4):
                            e.copy(out=o_, in_=i_), [p6[:, 0:512]], [ygT[:, :, tsl_(c3)]])
                sp = r.dma("sync", ygs_d.ap()[h], ygT, "ygsp")
                spills.append(sp)
            R2 = Reg(A, RA, RA_BYTES)
            ygg = [R2.alloc([16, 512], BF16) for _ in range(2)]
            wo = R2.alloc([8, 16, 128], BF16)
            ybuf = R2.alloc([8, 512], F32)
            xg = R2.alloc([8, 512], F32)
            r.dma("gpsimd", wo, retwout_d.ap().rearrange("m p (a b) -> p m a b", a=16), "wo")
            for j in range(16):
                ts("vector", wo[:, :, j, :], wo[:, :, j, :], gnwf[:, j:j + 1], None, ALU.mult)
            for g in range(NG):
                gs = slice(g * GT, (g + 1) * GT)
                yb = ygg[g % 2]
                for h in range(4):
                    r.dma("sync", yb[:, h * 4:(h + 1) * 4, :], ygs_d.ap()[h, :, :, gs], "ygg%d_%d" % (g % 2, h),
                          extra_deps=[spills[-4 + h]])
                r.dma("sync", xg, xT_d.ap()[b, :, :, gs], "xg")
                for m in range(8):
                    pb_ = ps(nb())
                    for j in range(16):
                        mm(pb_, wo[:, m, j, :], yb[:, j, :], j == 0, j == 15)
                    act(ybuf[:, m, :], pb_, AF.Copy)
                    act(sqb[:, m, :], pb_, AF.Square)
                postnorm(ybuf, P[:, 2, :], lambda c: xg[:, c, :], lambda c: xT[:, c, gs])

        spills = []

        def mla(b):
            P = prm[1][b]
            R = Reg(A, RA, RA_BYTES)
            Wr = R.alloc([20 * 1024 // 2], BF16)
            Wbase = RA
            win = A.view(Wbase, [8, 704], BF16)
            wkr = A.view(Wbase + 8 * 704 * 2, [8, 64], BF16)
            wuq = A.view(Wbase, [3, 1536], BF16)
            wukv = A.view(Wbase + 9216, [2, 2048], BF16)
            wqr = A.view(Wbase + 9216 + 8192, [3, 512], BF16)
            wout = A.view(Wbase, [8, 1024], BF16)
            Tbase = RA + 20 * 1024
            T = R.alloc([21 * 1024 // 2], BF16)
            hdn = A.view(Tbase, [8, 512], BF16)
            c32 = A.view(Tbase + 8192, [5, 512], F32)
            qTn = A.view(Tbase, [S], BF16)
            qTr = A.view(Tbase + 4096, [S], BF16)
            kTn = A.view(Tbase + 8192, [S], BF16)
            vtm = A.view(Tbase + 12288, [16, 128], BF16)
            pT = [A.view(Tbase + 16384 + i * 1024, [512], BF16) for i in range(5)]
            rden = rs
            ybuf = A.view(Tbase, [8, 512], F32)
            cqn = R.alloc([3, S], BF16)
            ckvn = R.alloc([2, S], BF16)
            krT = R.alloc([S], BF16)
            cos64 = R.alloc([S], F32)
            sin64 = R.alloc([S], F32)
            oT = R.alloc([8, S], BF16)
            rope_prep(b, RA + RA_BYTES - 32768)
            memset("gpsimd", krT[64:128], 0.0)
            r.dma("gpsimd", win, mwin_d.ap().rearrange("p (a b) -> p a b", a=8), "mw0")
            ts("vector", wkr[:, :, 0:32], win[:, :, 672:704], -1.0, None, ALU.mult)
            cp("gpsimd", wkr[:, :, 32:64], win[:, :, 640:672])
            for g in range(NG):
                gs = slice(g * GT, (g + 1) * GT)
                prenorm_a(lambda c: xT[:, c, gs])
                rope_slice(invf_mla, 64, cos64[0:64], sin64[0:64], RA + RA_BYTES - 32768, gs)
                prenorm_b(lambda c: xT[:, c, gs], P[:, 0, :], P[:, 1, :], hdn)
                for j in range(5):
                    pb_ = ps(nb())
                    for kc in range(8):
                        mm(pb_, win[:, kc, j * 128:(j + 1) * 128], hdn[:, kc, :], kc == 0, kc == 7)
                    act(c32[:, j, :], pb_, AF.Copy)
                    act(sqb[:, j, :], pb_, AF.Square)
                pk, pkr = ps(nb()), ps(nb())
                for kc in range(8):
                    mm(pk[0:64], win[:, kc, 640:704], hdn[:, kc, :], kc == 0, kc == 7)
                for kc in range(8):
                    mm(pkr[0:64], wkr[:, kc, :], hdn[:, kc, :], kc == 0, kc == 7)
                tt("vector", scr[0][0:64], pk[0:64], cos64[0:64, gs], ALU.mult)
                tt("vector", scr[1][0:64], pkr[0:64], sin64[0:64, gs], ALU.mult)
                tt("gpsimd", krT[0:64, gs], scr[0][0:64], scr[1][0:64], ALU.add)
                p7 = ps(7)
                for c in range(3):
                    mm(p7, ones, sqb[:, c, :], c == 0, c == 2)
                act(rs, p7, AF.Ln, scale=1.0 / 384, bias=eps6)
                act(rs, rs, AF.Exp, scale=-0.5)
                for c in range(3):
                    stt("vector", cqn[:, c, gs], c32[:, c, :], mnorm[:, c:c + 1], rs, ALU.mult, ALU.mult)
                p7 = ps(7)
                for c in range(2):
                    mm(p7, ones, sqb[:, 3 + c, :], c == 0, c == 1)
                act(rs, p7, AF.Ln, scale=1.0 / 256, bias=eps6)
                act(rs, rs, AF.Exp, scale=-0.5)
                for c in range(2):
                    stt("vector", ckvn[:, c, gs], c32[:, 3 + c, :], mnorm[:, 3 + c:4 + c], rs, ALU.mult, ALU.mult)
            r.dma("gpsimd", wuq, mwuq_d.ap().rearrange("p (a b) -> p a b", a=3), "mw1")
            r.dma("gpsimd", wukv, mwukv_d.ap().rearrange("p (a b) -> p a b", a=2), "mw2")
            for h in range(8):
                c0 = h * 192 + 128
                ts("vector", wqr[:, :, h * 64:h * 64 + 32], wuq[:, :, c0 + 32:c0 + 64], -1.0, None, ALU.mult)
                cp("gpsimd", wqr[:, :, h * 64 + 32:h * 64 + 64], wuq[:, :, c0:c0 + 32])
            scale = 192.0 ** -0.5
            memset("gpsimd", qTr[64:128], 0.0)
            zeros32 = sqb[:, 2:4, :].bitcast(F32).rearrange("p a b -> p (a b)")
            memset("gpsimd", zeros32, 0.0)
            for h in range(8):
                if b == 0 and 0 in layers:
                    d2d[("w1", 1, h)] = r.dma("gpsimd", w1s_d.ap()[1, h], w1r_d.ap()[1, h], "d2dw1_1")
                    d2d[("w2", 1, h)] = r.dma("gpsimd", w2s_d.ap()[1, h], w2r_d.ap()[1, h], "d2dw2_1")
                for g in range(NG):
                    gs = slice(g * GT, (g + 1) * GT)
                    pq = ps(nb())
                    for kc in range(3):
                        mm(pq, wuq[:, kc, h * 192:h * 192 + 128], cqn[:, kc, gs], kc == 0, kc == 2)
                    act(qTn[:, gs], pq, AF.Copy)
                    pk, pkr = ps(nb()), ps(nb())
                    for kc in range(3):
                        mm(pk[0:64], wuq[:, kc, h * 192 + 128:h * 192 + 192], cqn[:, kc, gs], kc == 0, kc == 2)
                    for kc in range(3):
                        mm(pkr[0:64], wqr[:, kc, h * 64:(h + 1) * 64], cqn[:, kc, gs], kc == 0, kc == 2)
                    tt("vector", scr[0][0:64], pk[0:64], cos64[0:64, gs], ALU.mult)
                    tt("vector", scr[1][0:64], pkr[0:64], sin64[0:64, gs], ALU.mult)
                    tt("gpsimd", qTr[0:64, gs], scr[0][0:64], scr[1][0:64], ALU.add)
                    pkn = ps(nb())
                    for kc in range(2):
                        mm(pkn, wukv[:, kc, h * 256:h * 256 + 128], ckvn[:, kc, gs], kc == 0, kc == 1)
                    act(kTn[:, gs], pkn, AF.Copy)
                    pv = ps(nb())
                    for ti in range(4):
                        t = g * 4 + ti
                        for kc in range(2):
                            mm(pv[:, ti * 128:(ti + 1) * 128], ckvn[:, kc, t * 128:(t + 1) * 128],
                               wukv[:, kc, h * 256 + 128:h * 256 + 256], kc == 0, kc == 1)
                    cp("vector", vtm[:, g * 4:(g + 1) * 4, :], pv.rearrange("p (a b) -> p a b", a=4))
                steps = [(g, kt) for g in range(NG) for kt in range(16)]
                sc_rot = [0]

                def scores(g, kt):
                    gs = slice(g * GT, (g + 1) * GT)
                    ksl = slice(kt * 128, (kt + 1) * 128)
                    pS = ps((0, 1, 2, 3, 7)[sc_rot[0] % 5])
                    sc_rot[0] += 1
                    mm(pS, kTn[:, ksl], qTn[:, gs], True, False)
                    mm(pS, krT[:, ksl], qTr[:, gs], False, True)
                    return pS
                pend = [scores(*steps[0]), scores(*steps[1]), scores(*steps[2]), scores(*steps[3])]
                for i, (g, kt) in enumerate(steps):
                    gs = slice(g * GT, (g + 1) * GT)
                    cur = pend.pop(0)
                    if i + 4 < len(steps):
                        pend.append(scores(*steps[i + 4]))
                    po, pd = (ps(4) if g % 2 == 0 else ps(6)), ps(5)
                    pt = pT[i % 5]
                    act(pt, cur, AF.Exp, scale=scale)
                    mm(po, vtm[:, kt, :], pt, kt == 0, kt == 15)
                    acc0 = scr[2]
                    if kt % 3 == 2:
                        mm(pd, ones, pt, kt == 2, False)
                    elif kt == 0:
                        cp("vector", acc0, pt)
                    else:
                        tt("vector", acc0, acc0, pt, ALU.add)
                    if kt == 15:
                        ah = sqb[:, 0, :]
                        cp("vector", ah, acc0)
                        mm(pd, ones, ah, False, True)
                        act(rden, pd, AF.Ln)
                        act(rden, rden, AF.Exp, scale=-1.0)
                        tt("vector", oT[:, h, gs], po, rden, ALU.mult)
            r.dma("gpsimd", wout, mwout_d.ap().rearrange("p (a b) -> p a b", a=8), "mw3")
            for g in range(NG):
                gs = slice(g * GT, (g + 1) * GT)
                for m in range(8):
                    pb_ = ps(nb())
                    for hh in range(8):
                        mm(pb_, wout[:, hh, m * 128:(m + 1) * 128], oT[:, hh, gs], hh == 0, hh == 7)
                    act(ybuf[:, m, :], pb_, AF.Copy)
                    act(sqb[:, m, :], pb_, AF.Square)
                postnorm(ybuf, P[:, 2, :], lambda c: xT[:, c, gs], lambda c: xT[:, c, gs])

        for b in range(nseq):
            if 0 in layers:
                ret = retention(b)
                if dbg == "hdn0":
                    pass
                elif dbg != "xmix0":
                    mlp(0, b)
            else:
                for g in range(NG):
                    gs = slice(g * GT, (g + 1) * GT)
                    r.dma("sync", xT[:, :, gs], xT_d.ap()[b, :, :, gs], "xld%d" % g)
            if 1 in layers and dbg not in ("hdn0", "xmix0", "x0"):
                if b == 0:
                    if 0 not in layers:
                        emit_d2d(0)
                        emit_d2d(1)
                mla(b)
                if dbg != "xmix1":
                    mlp(1, b)
            if dbg == "hdn0":
                for g in range(NG):
                    gs = slice(g * GT, (g + 1) * GT)
                    for c in range(8):
                        cp("vector", scr[c % 4], ret[:, c, gs])
                        r.dma("sync", out_d.ap()[b, :, c, gs], scr[c % 4], "od%d" % (c % 4))
            elif dbg is not None:
                for g in range(NG):
                    gs = slice(g * GT, (g + 1) * GT)
                    r.dma("sync", out_d.ap()[b, :, :, gs], xT[:, :, gs], "ost%d" % g)
        r.emit()
    return nc


def _host_layout(inp):
    f = np.float32
    sh = {}
    ada_w = np.asarray(inp["ada_w"], f)
    sh["adar"] = np.ascontiguousarray(ada_w.reshape(2, 8, 128, 12, 512).transpose(0, 3, 2, 1, 4)).reshape(2, 12, 128, 4096)
    ada_b = np.asarray(inp["ada_b"], f)
    ab = ada_b.reshape(2, 48, 128).transpose(0, 2, 1)
    sh["adab"] = np.ascontiguousarray(np.repeat(ab[:, :, :, None], 2, axis=3)).reshape(2, 128, 96)
    nw = np.asarray(inp["norm_w"], f)
    sh["normw"] = np.ascontiguousarray(nw.reshape(2, 4, 8, 128).transpose(0, 3, 1, 2)).reshape(2, 128, 32)
    wi = np.asarray(inp["ret_w_in"], f)[0]
    q = wi[:, 0:1024].reshape(1024, 4, 256)
    k = wi[:, 1024:2048].reshape(1024, 4, 256)
    v = wi[:, 2048:4096].reshape(1024, 4, 512)
    g = wi[:, 4096:6144].reshape(1024, 4, 512)
    wh = np.concatenate([q, k, v, g], axis=2)
    sh["retwin"] = np.ascontiguousarray(wh.reshape(8, 128, 4, 1536).transpose(2, 1, 0, 3)).reshape(4, 128, 8 * 1536)
    wo = np.asarray(inp["ret_w_out"], f)[0]
    sh["retwout"] = np.ascontiguousarray(wo.reshape(16, 128, 8, 128).transpose(2, 1, 0, 3)).reshape(8, 128, 2048)
    sh["gnw"] = np.ascontiguousarray(np.asarray(inp["ret_gn_w"], f).reshape(16, 128).T)
    sh["declog"] = np.ascontiguousarray(np.concatenate([np.asarray(inp["ret_decay_logit_fwd"], f)[0],
                                                        np.asarray(inp["ret_decay_logit_bwd"], f)[0]]).reshape(1, 8))
    w1 = np.asarray(inp["mlp_w1"], f)
    sh["w1r"] = np.ascontiguousarray(w1.reshape(2, 8, 128, 8, 512).transpose(0, 3, 2, 1, 4)).reshape(2, 8, 128, 4096)
    w2 = np.asarray(inp["mlp_w2"], f)
    sh["w2r"] = np.ascontiguousarray(w2.reshape(2, 32, 128, 8, 128).transpose(0, 3, 2, 1, 4)).reshape(2, 8, 128, 4096)
    mw = np.asarray(inp["mla_w_in"], f)[0]
    sh["mwin"] = np.ascontiguousarray(mw.reshape(8, 128, 704).transpose(1, 0, 2)).reshape(128, 8 * 704)
    mq = np.asarray(inp["mla_w_uq"], f)[0]
    sh["mwuq"] = np.ascontiguousarray(mq.reshape(3, 128, 1536).transpose(1, 0, 2)).reshape(128, 3 * 1536)
    mk = np.asarray(inp["mla_w_ukv"], f)[0]
    sh["mwukv"] = np.ascontiguousarray(mk.reshape(2, 128, 2048).transpose(1, 0, 2)).reshape(128, 2 * 2048)
    mo = np.asarray(inp["mla_w_out"], f)[0]
    sh["mwout"] = np.ascontiguousarray(mo.reshape(8, 128, 1024).transpose(1, 0, 2)).reshape(128, 8 * 1024)
    mn = np.zeros((128, 8), f)
    mn[:, 0:3] = np.asarray(inp["mla_q_norm"], f)[0].reshape(3, 128).T
    mn[:, 3:5] = np.asarray(inp["mla_kv_norm"], f)[0].reshape(2, 128).T
    sh["mnorm"] = mn
    ct = np.zeros((128, 4 * 128 + 8), f)
    p = np.arange(128, dtype=f)[:, None]
    n = np.arange(128, dtype=f)[None, :]
    ct[:, 0:128] = np.maximum(n - p, 0)
    ct[:, 128:256] = np.maximum(p - n, 0)
    ct[:, 256:384] = n + 1
    ct[:, 384:512] = 128 - n
    ct[:, 512] = 127 - p[:, 0]
    ct[:, 513] = p[:, 0]
    ct[:, 514] = (10000.0 ** (-np.arange(0, 256, 2, dtype=f) / 256)).astype(f)
    im = (10000.0 ** (-np.arange(0, 64, 2, dtype=f) / 64)).astype(f)
    ct[0:32, 515] = im
    ct[32:64, 515] = im
    sh["ctab"] = ct
    return sh


def _core_inputs(inp, core, shared):
    f = np.float32
    b0 = 2 * core
    x = np.asarray(inp["x"], f)[b0:b0 + 2]
    xT = np.ascontiguousarray(x.reshape(2, S, 8, 128).transpose(0, 3, 2, 1))
    c = np.asarray(inp["c"], f)[b0:b0 + 2]
    cT = np.ascontiguousarray(c.reshape(2, 8, 128).transpose(2, 1, 0))
    pos = np.ascontiguousarray(np.asarray(inp["positions"])[b0:b0 + 2].astype(np.int32))
    m = dict(shared)
    m["xT"] = xT
    m["cT"] = cT
    m["pos"] = pos
    return m


_NC_CACHE = {}


def kernel(**inputs):
    shared = _host_layout(inputs)
    if "nc" not in _NC_CACHE:
        _NC_CACHE["nc"] = build()
    nc = _NC_CACHE["nc"]
    in_maps = [_core_inputs(inputs, i, shared) for i in range(8)]
    res = run_bass_kernel_spmd(nc, in_maps, core_ids=list(range(8)))
    out = np.empty((16, S, D), np.float32)
    for i in range(8):
        oT = np.asarray(res.results[i]["outT"])
        out[2 * i:2 * i + 2] = oT.transpose(0, 3, 2, 1).reshape(2, S, D)
    return out
```
